# Optimizing a Trainium2 kernel written in Bass

```python
import math
import jax, jax.numpy as jnp
from jax import lax
import numpy as np

D_MODEL = 1024
BATCH = 16
SEQ = 256
DEPTH = 2
DEC_BATCH = 8
DEC_SEQ = 1024
PAST_LEN = 256

GRID_W = 64
MIX_WIDTH = D_MODEL
ATTN_HEADS = 4
ATTN_DH = 64
ATTN_WIDTH = ATTN_HEADS * 2 * ATTN_DH
CONV_CH = MIX_WIDTH // 4
CHUNK = 128
CHUNK_GROUPS = 4
CHUNK_CH = MIX_WIDTH // 4
CHUNK_GDIM = CHUNK_CH // CHUNK_GROUPS
D_FF = 2816
N_MOD = 9
Q_BLOCK = 128
ROPE_BASE = 10000.0
ROPE_FREQS = ATTN_DH // 4
QKV_W = ATTN_HEADS * 2 * ATTN_DH
IN_SPLITS = (QKV_W, 2 * QKV_W, 3 * QKV_W,
             3 * QKV_W + CONV_CH, 3 * QKV_W + 2 * CONV_CH, 3 * QKV_W + 3 * CONV_CH,
             3 * QKV_W + 3 * CONV_CH + CHUNK_CH)
IN_WIDTH = 3 * QKV_W + 3 * CONV_CH + 2 * CHUNK_CH

kernel_name = "hybrid_diffusion_prefix_trunk_step"


def rms_norm(x, g, eps=1e-6):
    xf = x.astype(jnp.float32)
    y = xf * lax.rsqrt(jnp.mean(xf * xf, axis=-1, keepdims=True) + eps)
    return (y * g.astype(jnp.float32)).astype(x.dtype)


def swiglu(h, w_gu, w_d):
    g, u = jnp.split(h @ w_gu, 2, axis=-1)
    return (jax.nn.silu(g) * u) @ w_d


def axial_rope_tables(n_tokens):
    n_rows = n_tokens // GRID_W
    row = jnp.repeat(jnp.arange(n_rows, dtype=jnp.float32), GRID_W)
    col = jnp.tile(jnp.arange(GRID_W, dtype=jnp.float32), n_rows)
    inv = ROPE_BASE ** (-jnp.arange(ROPE_FREQS, dtype=jnp.float32) / ROPE_FREQS)
    ang = jnp.concatenate([row[:, None] * inv, col[:, None] * inv], axis=-1)
    return jnp.cos(ang), jnp.sin(ang)


def apply_rope(x, cos, sin):
    cos = cos[:, None, :].astype(x.dtype)
    sin = sin[:, None, :].astype(x.dtype)
    x1, x2 = jnp.split(x, 2, axis=-1)
    return jnp.concatenate([x1 * cos - x2 * sin, x2 * cos + x1 * sin], axis=-1)


def diff_attention(q, k, v, lam):
    B, H, Sq = q.shape[:3]
    nb = Sq // Q_BLOCK
    qb = q.reshape(B, H, nb, Q_BLOCK, 2, ATTN_DH).transpose(2, 0, 1, 3, 4, 5)
    scale = ATTN_DH ** -0.5

    def one_block(q_blk):
        s = jnp.einsum('bhqmd,bhkmd->bhmqk', q_blk, k).astype(jnp.float32) * scale
        p = jax.nn.softmax(s, axis=-1)
        a = p[:, :, 0] - lam * p[:, :, 1]
        return jnp.einsum('bhqk,bhkd->bhqd', a.astype(v.dtype), v)

    o = lax.map(one_block, qb)
    return o.transpose(1, 2, 0, 3, 4).reshape(B, H, Sq, 2 * ATTN_DH)


def token_mixers(h, p, layer_idx, rope, ctx_kv):
    B, S, _ = h.shape
    z = h @ p['w_in']
    q, k, v, gb, gc, hc, u, vc = jnp.split(z, IN_SPLITS, axis=-1)

    q = q.reshape(B, S, ATTN_HEADS, 2, ATTN_DH).transpose(0, 2, 1, 3, 4)
    k = k.reshape(B, S, ATTN_HEADS, 2, ATTN_DH).transpose(0, 2, 1, 3, 4)
    v = v.reshape(B, S, ATTN_HEADS, 2 * ATTN_DH).transpose(0, 2, 1, 3)
    if rope is None:
        keys, vals = k, v
        new_kv = (k.reshape(B, ATTN_HEADS, S, 2 * ATTN_DH), v)
    else:
        cos, sin = rope
        q = apply_rope(q, cos, sin)
        k = apply_rope(k, cos, sin)
        ck, cv = ctx_kv
        P = ck.shape[2]
        keys = jnp.concatenate([ck.reshape(B, ATTN_HEADS, P, 2, ATTN_DH).astype(k.dtype), k], axis=2)
        vals = jnp.concatenate([cv.astype(v.dtype), v], axis=2)
        new_kv = None
    lam_p = p['attn_lam'].astype(jnp.float32)
    lam_init = 0.8 - 0.6 * math.exp(-0.3 * layer_idx)
    lam = jnp.exp(jnp.sum(lam_p[0] * lam_p[1])) - jnp.exp(jnp.sum(lam_p[2] * lam_p[3])) + lam_init
    o = diff_attention(q, keys, vals, lam)
    o = rms_norm(o, p['attn_subln_g'][:, None, :]) * (1.0 - lam_init)
    y_attn = o.transpose(0, 2, 1, 3).reshape(B, S, ATTN_WIDTH)

    xc = gc * hc
    xp = jnp.pad(xc, ((0, 0), (1, 1), (0, 0)))
    w = p['conv_w']
    y_conv = gb * (w[0] * xp[:, :-2] + w[1] * xp[:, 1:-1] + w[2] * xp[:, 2:])

    nck = S // CHUNK
    vr = vc.reshape(B, nck, CHUNK, CHUNK_GROUPS, CHUNK_GDIM)
    sp = jnp.einsum('gpq,bnqgc->bnpgc', p['chunk_ws'], vr) + p['chunk_b'].T[None, None, :, :, None]
    y_chunk = u * sp.reshape(B, S, CHUNK_CH)

    y = jnp.concatenate([y_attn, y_conv, y_chunk], axis=-1) @ p['w_out']
    return y, new_kv


def trunk_layer(x, cond, p, layer_idx, rope, ctx_kv):
    m = jax.nn.silu(cond) @ p['w_mod'] + p['b_mod']
    sh1, sc1, g1, sh2, sc2, g2, sh3, sc3, g3 = jnp.split(m[:, None, :], N_MOD, axis=-1)
    h = rms_norm(x, p['norm_g'][0]) * (1.0 + sc1) + sh1
    x = x + 0.5 * g1 * swiglu(h, p['ffn1_w_gu'], p['ffn1_w_d'])
    h = rms_norm(x, p['norm_g'][1]) * (1.0 + sc2) + sh2
    y, new_kv = token_mixers(h, p, layer_idx, rope, ctx_kv)
    x = x + g2 * y
    h = rms_norm(x, p['norm_g'][2]) * (1.0 + sc3) + sh3
    x = x + 0.5 * g3 * swiglu(h, p['ffn2_w_gu'], p['ffn2_w_d'])
    return x, new_kv


def setup_inputs(seed: int = 0) -> dict:
    key = jax.random.key(seed)
    ks = jax.random.split(key, 24)
    f32 = jnp.float32

    def nrm(k, shape, scale):
        return jax.random.normal(k, shape, f32) * scale

    D, F = D_MODEL, D_FF
    return {
        "x_prompt": nrm(ks[0], (BATCH, SEQ, D), 1.0),
        "x_sample": nrm(ks[1], (DEC_BATCH, DEC_SEQ, D), 1.0),
        "cache_k": nrm(ks[2], (DEC_BATCH, DEPTH, ATTN_HEADS, PAST_LEN, 2 * ATTN_DH), 1.0),
        "cache_v": nrm(ks[3], (DEC_BATCH, DEPTH, ATTN_HEADS, PAST_LEN, 2 * ATTN_DH), 1.0),
        "c": nrm(ks[4], (DEC_BATCH, D), 1.0),
        "c_ctx": nrm(ks[5], (D,), 1.0),
        "w_mod": nrm(ks[6], (DEPTH, D, N_MOD * D), 0.5 * D ** -0.5),
        "b_mod": nrm(ks[7], (DEPTH, N_MOD * D), 0.02),
        "norm_g": 1.0 + nrm(ks[8], (DEPTH, 3, D), 0.02),
        "ffn1_w_gu": nrm(ks[9], (DEPTH, D, 2 * F), D ** -0.5),
        "ffn1_w_d": nrm(ks[10], (DEPTH, F, D), F ** -0.5),
        "ffn2_w_gu": nrm(ks[11], (DEPTH, D, 2 * F), D ** -0.5),
        "ffn2_w_d": nrm(ks[12], (DEPTH, F, D), F ** -0.5),
        "w_in": nrm(ks[13], (DEPTH, D, IN_WIDTH), D ** -0.5),
        "w_out": nrm(ks[14], (DEPTH, MIX_WIDTH, D), MIX_WIDTH ** -0.5),
        "attn_lam": nrm(ks[15], (DEPTH, 4, ATTN_DH), 0.1),
        "attn_subln_g": 1.0 + nrm(ks[16], (DEPTH, ATTN_HEADS, 2 * ATTN_DH), 0.02),
        "conv_w": nrm(ks[17], (DEPTH, 3, CONV_CH), 3 ** -0.5),
        "chunk_ws": nrm(ks[18], (DEPTH, CHUNK_GROUPS, CHUNK, CHUNK), CHUNK ** -0.5),
        "chunk_b": 1.0 + nrm(ks[19], (DEPTH, CHUNK_GROUPS, CHUNK), 0.1),
        "final_norm_g": 1.0 + nrm(ks[20], (D,), 0.02),
    }


def reference(x_prompt, x_sample, cache_k, cache_v, c, c_ctx, w_mod, b_mod, norm_g,
              ffn1_w_gu, ffn1_w_d, ffn2_w_gu, ffn2_w_d, w_in, w_out, attn_lam,
              attn_subln_g, conv_w, chunk_ws, chunk_b, final_norm_g):
    def layer_params(l):
        return dict(w_mod=w_mod[l], b_mod=b_mod[l], norm_g=norm_g[l],
                    ffn1_w_gu=ffn1_w_gu[l], ffn1_w_d=ffn1_w_d[l],
                    ffn2_w_gu=ffn2_w_gu[l], ffn2_w_d=ffn2_w_d[l],
                    w_in=w_in[l], w_out=w_out[l], attn_lam=attn_lam[l],
                    attn_subln_g=attn_subln_g[l], conv_w=conv_w[l],
                    chunk_ws=chunk_ws[l], chunk_b=chunk_b[l])

    xp = x_prompt
    ks_list, vs_list = [], []
    cond_ctx = c_ctx[None, :]
    for l in range(DEPTH):
        xp, (k_l, v_l) = trunk_layer(xp, cond_ctx, layer_params(l), l, None, None)
        ks_list.append(k_l)
        vs_list.append(v_l)
    y_prompt = rms_norm(xp, final_norm_g)
    new_k = jnp.stack(ks_list, axis=1)
    new_v = jnp.stack(vs_list, axis=1)

    xs = x_sample
    rope = axial_rope_tables(xs.shape[1])
    for l in range(DEPTH):
        xs, _ = trunk_layer(xs, c, layer_params(l), l, rope, (cache_k[:, l], cache_v[:, l]))
    y_sample = rms_norm(xs, final_norm_g)

    return (y_prompt, y_sample, new_k, new_v)
```

```python
import contextlib
import math
import numpy as np
import concourse.bass as bass
import concourse.mybir as mybir
from concourse.bass_utils import run_bass_kernel_spmd

F32 = mybir.dt.float32
BF16 = mybir.dt.bfloat16
AF = mybir.ActivationFunctionType
ALU = mybir.AluOpType
AX = mybir.AxisListType

NCORES = 8
D = 1024
FF = 2816
L = 2
H = 4
TP = 512
TS = 1024
T = TP + TS
PAST = 256
NKC = 8
NJ = 22
NTILE = 12
NTT = 3
EPS = 1e-6


class Buf:
    __slots__ = ("name", "w", "r")

    def __init__(self, name):
        self.name = name
        self.w = None
        self.r = []


class DmaSem:
    __slots__ = ("name", "count", "handle", "group")

    def __init__(self, name, group=False):
        self.name = name
        self.count = 0
        self.handle = None
        self.group = group


class Op:
    __slots__ = ("eng", "fn", "deps", "signal", "val", "dma", "tag")

    def __init__(self, eng, fn, dma, tag):
        self.eng = eng
        self.fn = fn
        self.deps = []
        self.signal = False
        self.val = 0
        self.dma = dma
        self.tag = tag


ENGS = ("pe", "act", "dve", "pool", "sp")
EPOCH = 30000


class Prog:
    def __init__(self, same_engine_sync=True):
        self.ops = {e: [] for e in ENGS}
        self.dmasems = []
        self.same_engine_sync = same_engine_sync
        self.all_ops = []

    def dmasem(self, name, group=False):
        s = DmaSem(name, group)
        self.dmasems.append(s)
        return s

    def fence(self, bufs):
        deps = {}
        for b in bufs:
            if b.w is not None:
                deps[id(b.w)] = b.w
            for r in b.r:
                deps[id(r)] = r
        return list(deps.values())

    def op(self, eng, fn, reads=(), writes=(), dma=None, tag="", extra=()):
        o = Op(eng, fn, dma, tag)
        deps = {}
        for b in reads:
            if b.w is not None:
                deps[id(b.w)] = b.w
        for b in writes:
            if b.w is not None:
                deps[id(b.w)] = b.w
            for r in b.r:
                deps[id(r)] = r
        for d in extra:
            deps[id(d)] = d
        o.deps = list(deps.values())
        for b in reads:
            if dma is None:
                b.r = [r for r in b.r if not (r.dma is None and r.eng == eng)]
            b.r.append(o)
        for b in writes:
            b.w = o
            b.r = []
        if dma is not None:
            dma.count += 16
            o.val = dma.count
            o.signal = True
        self.ops[eng].append(o)
        self.all_ops.append(o)
        return o

    def _needs_wait(self, o, d):
        if d.dma is not None:
            return True
        if d.eng != o.eng:
            return True
        if o.dma is not None:
            return True
        if d.eng == "pe":
            return False
        return self.same_engine_sync

    def emit(self, nc, es, final_waits=()):
        for o in self.all_ops:
            for d in o.deps:
                if self._needs_wait(o, d):
                    d.signal = True
        counts = {}
        for e in ENGS:
            c = 0
            for o in self.ops[e]:
                if o.dma is None and o.signal:
                    c += 1
                    o.val = c
            counts[e] = c
        engsems = {}
        for e in ENGS:
            n_ep = (counts[e] + EPOCH - 1) // EPOCH
            engsems[e] = [es.enter_context(nc.semaphore(f"c_{e}_{i}")) for i in range(n_ep)]
        for s in self.dmasems:
            if s.count > 0:
                s.handle = es.enter_context(nc.semaphore(f"d_{s.name}"))
        block = es.enter_context(nc.Block())

        def semval(d):
            if d.dma is not None:
                return d.dma.handle, (d.dma.count if d.dma.group else d.val)
            ep = (d.val - 1) // EPOCH
            return engsems[d.eng][ep], d.val - ep * EPOCH

        def run(ename, engine):
            waited = {}
            for o in self.ops[ename]:
                for d in o.deps:
                    if not self._needs_wait(o, d):
                        continue
                    sem, val = semval(d)
                    k = sem.num
                    if waited.get(k, 0) >= val:
                        continue
                    engine.wait_ge(sem, val)
                    waited[k] = val
                ins = o.fn(engine)
                if o.signal:
                    if o.dma is not None:
                        ins.then_inc(o.dma.handle, 16)
                    else:
                        ep = (o.val - 1) // EPOCH
                        ins.then_inc(engsems[ename][ep], 1)
            if ename == "sp":
                for s in final_waits:
                    if s.count > 0:
                        engine.wait_ge(s.handle, s.count)

        block.sync(lambda eng: run("sp", eng))
        block.tensor(lambda eng: run("pe", eng))
        block.scalar(lambda eng: run("act", eng))
        block.vector(lambda eng: run("dve", eng))
        block.gpsimd(lambda eng: run("pool", eng))
        return counts


IN_SPECS = [
    ("xp", [TP, D]), ("xs", [TS, D]), ("ck", [L, H, PAST, 128]), ("cv", [L, H, PAST, 128]),
    ("condT", [128, NKC, 2]),
    ("w_mod", [L, D, 9 * D]), ("gu1", [L, D, 2 * FF]), ("d1", [L, FF, D]),
    ("gu2", [L, D, 2 * FF]), ("d2", [L, FF, D]), ("w_in", [L, D, FF]), ("w_out", [L, D, D]),
    ("bmodT", [128, L, 72]), ("normgT", [128, L, 3, NKC]), ("fnormT", [128, NKC]),
    ("lamA", [128, L * 2, 64]), ("lamB", [128, L * 2, 64]), ("sublnT", [128, L, H]),
    ("convT", [128, L, 2, 3]), ("wsT", [128, L, 4, 128]), ("cbT", [128, L, 4]),
    ("ident", [128, 128]), ("ones", [128, 128]),
    ("cosT", [128, 8, 32]), ("sinT", [128, 8, 32]), ("nsinT", [128, 8, 32]),
]
OUT_SPECS = [("yp", [TP, D]), ("ys", [TS, D]), ("nk", [2, L, H, 256, 128]), ("nv", [2, L, H, 256, 128])]


def build_program(dbg_names=(), stop_after=None):
    nc = bass.Bass("TRN2", target_bir_lowering=False)
    din = {n: nc.dram_tensor(n, s, F32, kind="ExternalInput").ap() for n, s in IN_SPECS}
    dout = {n: nc.dram_tensor(n, s, F32, kind="ExternalOutput").ap() for n, s in OUT_SPECS}
    ddbg = {}
    P = Prog()
    es = contextlib.ExitStack()
    with es:
        def sb(name, shape, dt):
            return es.enter_context(nc.sbuf_tensor("s_" + name, shape, dt))

        xT = sb("xT", [128, NKC, T], F32)
        hT = sb("hT", [128, NKC, T], BF16)
        Rg = sb("Rg", [128, NJ * T], BF16)
        Aslot = [sb(f"A{i}", [128, NKC, 512], BF16) for i in range(3)]
        Bslot = [sb(f"Bw{i}", [128, NJ, 128], BF16) for i in range(3)]
        scr = sb("scr", [128, 4, 512], F32)
        rstd = sb("rstd", [128, NTT, 512], F32)
        ident = sb("ident", [128, 128], F32)
        identb = sb("identb", [128, 128], BF16)
        onesb = sb("onesb", [128, 128], BF16)
        condT = sb("condT", [128, NKC, 2], F32)
        scondT = sb("scondT", [128, NKC, 2], BF16)
        mT = sb("mT", [128, L, 72, 2], F32)
        aN = sb("aN", [128, L, 3, NKC, 2], F32)
        gN = sb("gN", [128, L, 3, NKC, 2], F32)
        bmodT = sb("bmodT", [128, L, 72], F32)
        normgT = sb("normgT", [128, L, 3, NKC], F32)
        fnormT = sb("fnormT", [128, NKC], F32)
        lamA = sb("lamA", [128, L * 2, 64], F32)
        lamB = sb("lamB", [128, L * 2, 64], F32)
        lamS = sb("lamS", [128, 8], F32)
        sublnT = sb("sublnT", [128, L, H], F32)
        gsub = sb("gsub", [128, L, H], F32)
        convT = sb("convT", [128, L, 2, 3], F32)
        wsT = sb("wsT", [128, L, 4, 128], BF16)
        cbT = sb("cbT", [128, L, 4], F32)
        cosT = sb("cosT", [128, 8, 32], F32)
        sinT = sb("sinT", [128, 8, 32], F32)
        nsinT = sb("nsinT", [128, 8, 32], F32)
        ps = [es.enter_context(nc.psum_tensor(f"ps{i}", [128, 512], F32)) for i in range(8)]

        actT = Rg[:, :].rearrange("p (j t) -> p j t", j=NJ)
        qT = Rg[:, 0:6144].rearrange("p (h t) -> p h t", h=H)
        kTs = Rg[:, 6144:11264].rearrange("p (h t) -> p h t", h=H)
        kTp = Rg[:, 11264:13312].rearrange("p (h t) -> p h t", h=H)
        Vs = Rg[:, 13312:18432].rearrange("p (c f) -> p c f", c=10)
        Vp = Rg[:, 18432:20480].rearrange("p (c f) -> p c f", c=4)
        uvbf = Rg[:, 20480:26624].rearrange("p (c f) -> p c f", c=NTILE)
        xc = Rg[:, 26624:32768].bitcast(F32).rearrange("p (c t) -> p c t", c=2)
        b0 = Bslot[0][:, :, :].rearrange("p a b -> p (a b)")
        b1 = Bslot[1][:, :, :].rearrange("p a b -> p (a b)")
        b2 = Bslot[2][:, :, :].rearrange("p a b -> p (a b)")
        qpad = [b0[:, i * 1024:(i + 1) * 1024].rearrange("p (m t) -> p m t", m=2) for i in range(2)]
        ych = [b0[:, 2048 + i * 256:2048 + (i + 1) * 256] for i in range(2)]
        Ebuf = [b1[:, i * 512:(i + 1) * 512] for i in range(4)]
        tokbf = Ebuf
        sqo_all = Rg[:, 32768:33792]
        asc_all = b2[:, 0:2048].bitcast(F32)
        asc = [b2[:, i * 1024:(i + 1) * 1024].bitcast(F32) for i in range(2)]
        xin = [scr[:, 0:2, :].rearrange("p a b -> p (a b)"), scr[:, 2:4, :].rearrange("p a b -> p (a b)")]

        BxT = [[Buf(f"xT{k}_{t}") for t in range(NTT)] for k in range(NKC)]
        BhT = [[Buf(f"hT{k}_{t}") for t in range(NTT)] for k in range(NKC)]
        Bact = [[Buf(f"act{j}_{t}") for t in range(NTT)] for j in range(NJ)]
        BA = [[Buf(f"A{i}_{h}") for h in range(2)] for i in range(3)]
        BB = [Buf(f"Bw{i}") for i in range(3)]
        Bscr = [Buf(f"scr{i}") for i in range(4)]
        Brstd = [Buf(f"rstd{i}") for i in range(NTT)]
        Bps = [Buf(f"ps{i}") for i in range(8)]
        Bc = {n: Buf(n) for n in ("ident", "identb", "onesb", "condT", "scondT", "mT", "aN", "gN", "bmodT", "normgT",
                                  "fnormT", "lamA", "lamB", "lamS", "sublnT", "gsub", "convT", "wsT", "cbT",
                                  "cosT", "sinT", "nsinT")}
        BqT = [Buf(f"qT{i}") for i in range(NTILE)]
        BkT = [Buf(f"kT{i}") for i in range(NTILE)]
        BkTc = Buf("kTc")
        Bckst = [Buf(f"ckst{i}") for i in range(2)]
        Bxin = [Buf(f"xin{i}") for i in range(3)]
        BVs = [Buf(f"Vs{i}") for i in range(10)]
        BVp = [Buf(f"Vp{i}") for i in range(4)]
        Bu = [Buf(f"u{i}") for i in range(NTILE)]
        Bvc = [Buf(f"vc{i}") for i in range(NTILE)]
        Bxc = [[Buf(f"xc{c}_{t}") for t in range(NTT)] for c in range(2)]
        Bqpad = [Buf(f"qpad{i}") for i in range(2)]
        Bych = [Buf(f"ych{i}") for i in range(2)]
        BE = [Buf(f"E{i}") for i in range(4)]
        Btok = BE
        Bsqo = [Buf(f"sqo{i}") for i in range(4)]
        Bascq = [Buf(f"ascq{i}") for i in range(4)]
        Basc = [Buf(f"asc{i}") for i in range(2)]
        all_act_bufs = [b for row in Bact for b in row]
        all_mix_R_bufs = Bxin + Bckst + BqT + BkT + [BkTc] + BVs + BVp + Bu + Bvc + [b for row in Bxc for b in row] + Bsqo
        all_mix_B_bufs = Bqpad + Bych + BE + Basc + Bascq

        SA = [[P.dmasem(f"A{i}_{h}") for h in range(2)] for i in range(3)]
        SB = [P.dmasem(f"B{i}") for i in range(3)]
        Sconst = P.dmasem("const", group=True)
        Sconst0 = P.dmasem("const0", group=True)
        Sx = [P.dmasem(f"x{i}") for i in range(3)]
        Sost = [P.dmasem(f"ost{i}") for i in range(8)]
        Sconstp = P.dmasem("constp", group=True)
        Sscr = [P.dmasem(f"scr{i}") for i in range(4)]
        Sout = [P.dmasem(f"out{i}") for i in range(4)]
        Scachev = [P.dmasem(f"cachev{i}") for i in range(2)]
        Sck = [P.dmasem(f"cachek{i}") for i in range(2)]
        out_sems = list(Sout) + list(Sost)

        class Banks:
            def __init__(self):
                self.held = set()
                self.rr = 0

            def get(self, hold=False):
                for _ in range(16):
                    b = self.rr
                    self.rr = (self.rr + 1) % 8
                    if b not in self.held:
                        if hold:
                            self.held.add(b)
                        return b
                raise RuntimeError("no psum bank")

            def release(self, b):
                self.held.discard(b)

        banks = Banks()
        state = {"A": 0, "B": 0, "scr": 0, "E": 0, "tok": 0, "alt": 0, "sb": 0, "aset": 0}

        def next_A():
            i = state["A"]
            state["A"] = (i + 1) % 3
            return i

        def next_B():
            i = state["B"]
            state["B"] = (i + 1) % 3
            return i

        def next_scr():
            i = state["scr"]
            state["scr"] = (i + 1) % 4
            return i

        def tt_of_tile(i):
            return i // 4

        def cond_of_tt(tt):
            return 0 if tt == 0 else 1

        def mm(out, lhsT, rhs, start, stop, reads, writes, extra=()):
            P.op("pe", lambda e: e.matmul(out, lhsT, rhs, start=start, stop=stop), reads=reads, writes=writes, extra=extra)

        def tr(out, in_, idt, reads, writes, extra=()):
            P.op("pe", lambda e: e.transpose(out, in_, idt), reads=reads, writes=writes, extra=extra)

        def act(out, in_, func, reads, writes, bias=None, scale=None, extra=()):
            kw = {}
            if bias is not None:
                kw["bias"] = bias
            if scale is not None:
                kw["scale"] = scale
            P.op("act", lambda e: e.activation(out=out, in_=in_, func=func, **kw), reads=reads, writes=writes, extra=extra)

        def dve_tt(out, in0, in1, op, reads, writes, extra=()):
            P.op("dve", lambda e: e.tensor_tensor(out=out, in0=in0, in1=in1, op=op), reads=reads, writes=writes, extra=extra)

        def dve_ts(out, in0, s1, s2, op0, op1, reads, writes, extra=()):
            if s2 is None:
                P.op("dve", lambda e: e.tensor_scalar(out=out, in0=in0, scalar1=s1, scalar2=None, op0=op0), reads=reads, writes=writes, extra=extra)
            else:
                P.op("dve", lambda e: e.tensor_scalar(out=out, in0=in0, scalar1=s1, scalar2=s2, op0=op0, op1=op1), reads=reads, writes=writes, extra=extra)

        def dve_stt(out, in0, scalar, in1, op0, op1, reads, writes, extra=()):
            P.op("dve", lambda e: e.scalar_tensor_tensor(out=out, in0=in0, scalar=scalar, in1=in1, op0=op0, op1=op1), reads=reads, writes=writes, extra=extra)

        def dve_copy(out, in_, reads, writes, extra=()):
            P.op("dve", lambda e: e.tensor_copy(out=out, in_=in_), reads=reads, writes=writes, extra=extra)

        def dve_recip(out, in_, reads, writes):
            P.op("dve", lambda e: e.reciprocal(out=out, in_=in_), reads=reads, writes=writes)

        def dma(q, out, in_, reads, writes, sem, extra=()):
            return P.op(q, lambda e: e.dma_start(out=out, in_=in_), reads=reads, writes=writes, dma=sem, extra=extra)

        dbg_n = [0]

        def dump(name, ap, reads):
            if name not in dbg_names:
                return
            t = nc.dram_tensor("dbg_" + name, list(ap.shape), ap.dtype, kind="ExternalOutput").ap()
            ddbg[name] = t
            s = P.dmasem(f"dbg{dbg_n[0]}")
            dbg_n[0] += 1
            out_sems.append(s)
            dma("sp", t, ap, reads, [], s)

        for n, t in (("ident", ident), ("condT", condT)):
            dma("sp", t[:], din[n], [], [Bc[n]], Sconst0)
        for n, t in (("bmodT", bmodT), ("normgT", normgT), ("fnormT", fnormT),
                     ("lamA", lamA), ("lamB", lamB), ("sublnT", sublnT), ("convT", convT), ("cbT", cbT),
                     ("cosT", cosT), ("sinT", sinT), ("nsinT", nsinT)):
            dma("act", t[:], din[n], [], [Bc[n]], Sconst)
        xops = []
        xin_all = Rg[:, 0:24576].bitcast(F32).rearrange("p (n d) -> p n d", n=NTILE)
        for g in range(3):
            srcx = din["xp"] if g == 0 else din["xs"][(g - 1) * 512:g * 512, :]
            xops.append(dma("sp", xin_all[:, 4 * g:4 * g + 4, :], srcx.rearrange("(n p) d -> p n d", p=128), [], [Bxin[g]], Sx[g]))
        dma("pool", identb[:], din["ident"], [], [Bc["identb"]], Sconstp)
        dma("pool", onesb[:], din["ones"], [], [Bc["onesb"]], Sconstp)
        dma("pool", wsT[:], din["wsT"], [], [Bc["wsT"]], Sconstp)

        act(scondT[:], condT[:], AF.Silu, [Bc["condT"]], [Bc["scondT"]])
        lam_init = [0.8 - 0.6 * math.exp(-0.3 * l) for l in range(L)]

        def lam_setup():
            dve_tt(lamA[:], lamA[:], lamB[:], ALU.mult, [Bc["lamA"], Bc["lamB"]], [Bc["lamA"]])
            P.op("dve", lambda e: e.tensor_reduce(out=lamS[:, 0:4], in_=lamA[:], axis=AX.X, op=ALU.add), reads=[Bc["lamA"]], writes=[Bc["lamS"]])
            act(lamS[:, 0:4], lamS[:, 0:4], AF.Exp, [Bc["lamS"]], [Bc["lamS"]])
            for l in range(L):
                dve_tt(lamS[:, 4 + l:5 + l], lamS[:, 2 * l + 1:2 * l + 2], lamS[:, 2 * l:2 * l + 1], ALU.subtract, [Bc["lamS"]], [Bc["lamS"]])
                dve_ts(lamS[:, 4 + l:5 + l], lamS[:, 4 + l:5 + l], -lam_init[l], None, ALU.add, None, [Bc["lamS"]], [Bc["lamS"]])
                dve_ts(gsub[:, l, :], sublnT[:, l, :], 1.0 - lam_init[l], None, ALU.mult, None, [Bc["sublnT"]], [Bc["gsub"]])

        def x_tiles(tiles):
            for i in tiles:
                g = i // 4
                tt = tt_of_tile(i)
                for half in range(2):
                    b = banks.get()
                    for q in range(4):
                        kc = half * 4 + q
                        tr(ps[b][:, q * 128:(q + 1) * 128], xin_all[:, i, kc * 128:(kc + 1) * 128], ident[:], [Bxin[g], Bc["ident"]], [Bps[b]])
                    dst = xT[:, half * 4:(half + 1) * 4, i * 128:(i + 1) * 128]
                    srcv = ps[b][:].rearrange("p (a t) -> p a t", a=4)
                    wb = [BxT[half * 4 + q][tt] for q in range(4)]
                    if half == 0:
                        act(dst, srcv, AF.Copy, [Bps[b]], wb)
                    else:
                        dve_copy(dst, srcv, [Bps[b]], wb)

        def load_A(src2d, c0, ncols, dst_c0=0, slot=None, halves=(0, 1), extra=()):
            if slot is None:
                slot = next_A()
            srcv = src2d[:, c0:c0 + ncols].rearrange("(k p) f -> p k f", p=128)
            dma("pool", Aslot[slot][:, :, dst_c0:dst_c0 + ncols], srcv, [], [BA[slot][h] for h in halves], SA[slot][halves[0]], extra=extra)
            return slot

        mod_bank = {}
        Bmt = [[Buf(f"mT{l}_{i}") for i in range(3)] for l in range(L)]
        Bmg = [[Buf(f"mG{l}_{i}") for i in range(3)] for l in range(L)]

        def mod_group(l, gq, extra=()):
            if l not in mod_bank:
                mod_bank[l] = banks.get(hold=True)
            bm = mod_bank[l]
            s_ = load_A(din["w_mod"][l], gq * 512, 512, extra=extra)
            for jj in range(4):
                j = gq * 4 + jj
                for kc in range(NKC):
                    mm(ps[bm][:, 2 * j:2 * j + 2], Aslot[s_][:, kc, jj * 128:(jj + 1) * 128], scondT[:, kc, :], kc == 0, kc == NKC - 1,
                       [BA[s_][jj // 2], Bc["scondT"]], [Bps[bm]])

        def mod_finalize(l, i, part="ag"):
            bm = mod_bank[l]
            pv = ps[bm][:, 0:144].rearrange("p (j c) -> p j c", c=2)
            if "a" in part:
                for c in range(2):
                    dve_tt(mT[:, l, 24 * i:24 * i + 16, c], pv[:, 24 * i:24 * i + 16, c], bmodT[:, l, 24 * i:24 * i + 16], ALU.add,
                           [Bps[bm], Bc["bmodT"]], [Bmt[l][i]])
                for c in range(2):
                    dve_stt(aN[:, l, i, :, c], mT[:, l, (3 * i + 1) * 8:(3 * i + 2) * 8, c], 1.0, normgT[:, l, i, :], ALU.add, ALU.mult,
                            [Bmt[l][i], Bc["normgT"]], [Bmt[l][i]])
            if "g" in part:
                for c in range(2):
                    dve_tt(mT[:, l, 24 * i + 16:24 * i + 24, c], pv[:, 24 * i + 16:24 * i + 24, c], bmodT[:, l, 24 * i + 16:24 * i + 24], ALU.add,
                           [Bps[bm], Bc["bmodT"]], [Bmg[l][i]])
                dve_ts(gN[:, l, i, :, :], mT[:, l, (3 * i + 2) * 8:(3 * i + 3) * 8, :], 0.5 if i != 1 else 1.0, None, ALU.mult, None, [Bmg[l][i]], [Bmg[l][i]])
                if i == 2:
                    banks.release(bm)

        def norm_stats(tt, b):
            xb = [BxT[k][tt] for k in range(NKC)]
            hb = [BhT[k][tt] for k in range(NKC)]
            act(hT[:, :, tt * 512:(tt + 1) * 512], xT[:, :, tt * 512:(tt + 1) * 512], AF.Square, xb, hb)
            for kc in range(NKC):
                mm(ps[b][:], onesb[:], hT[:, kc, tt * 512:(tt + 1) * 512], kc == 0, kc == NKC - 1, [Bc["onesb"], BhT[kc][tt]], [Bps[b]])

        def norm_rstd(tt, b):
            act(rstd[:, tt, :], ps[b][:], AF.Ln, [Bps[b]], [Brstd[tt]], bias=EPS, scale=1.0 / D)
            act(rstd[:, tt, :], rstd[:, tt, :], AF.Exp, [Brstd[tt]], [Brstd[tt]], scale=-0.5)

        stat = {"banks": None, "pend": [], "mode": None}

        def stats_begin(mode):
            stat["banks"] = [banks.get(hold=True) for _ in range(NTT)]
            stat["mode"] = mode
            stat["pend"] = []
            stat["n"] = 0

        def stats_update(dc, tt):
            b = stat["banks"][tt]
            if stat["mode"] == "h":
                dst = hT[:, dc, tt * 512:(tt + 1) * 512]
                db = [BhT[dc][tt]]
            else:
                k = stat["n"] % 2
                stat["n"] += 1
                dst = Ebuf[k][:, :]
                db = [BE[k]]
            act(dst, xT[:, dc, tt * 512:(tt + 1) * 512], AF.Square, [BxT[dc][tt]], db)
            stat["pend"].append(lambda: mm(ps[b][:], onesb[:], dst, dc == 0, dc == NKC - 1, [Bc["onesb"]] + db, [Bps[b]]))

        def stats_flush(keep=0):
            while len(stat["pend"]) > keep:
                stat["pend"].pop(0)()

        def norm_phase(l, i_norm, final=False, fr=(), after_tt=None, drip_tail=False):
            if stat["banks"] is not None:
                stats_flush()
                bks = stat["banks"]
                stat["banks"] = None
                pre_stats = True
            else:
                bks = [banks.get(hold=True) for _ in range(NTT)]
                norm_stats(0, bks[0])
                norm_stats(1, bks[1])
                norm_stats(2, bks[2])
                pre_stats = False
            for tt in range(NTT):
                if drip_tail and tt >= 1:
                    norm_rstd(tt, bks[tt])
                    banks.release(bks[tt])
                    nq.extend(norm_chunk_ops(l, i_norm, tt))
                    continue
                norm_tt(l, i_norm, tt, bks[tt], final=final)
                if after_tt is not None:
                    after_tt(tt)

        def norm_chunk_ops(l, i_norm, tt):
            c = cond_of_tt(tt)

            def mk(kc):
                def go():
                    s_ = next_scr()
                    dve_stt(scr[:, s_, :], xT[:, kc, tt * 512:(tt + 1) * 512], aN[:, l, i_norm, kc, c:c + 1], rstd[:, tt, :], ALU.mult, ALU.mult,
                            [BxT[kc][tt], Bmt[l][i_norm], Brstd[tt]], [Bscr[s_]])
                    bias = mT[:, l, (3 * i_norm) * 8 + kc, c:c + 1]
                    dst = hT[:, kc, tt * 512:(tt + 1) * 512]
                    if kc == NKC - 1:
                        dve_ts(dst, scr[:, s_, :], bias, None, ALU.add, None, [Bscr[s_], Bmt[l][i_norm]], [BhT[kc][tt]])
                    else:
                        act(dst, scr[:, s_, :], AF.Identity, [Bscr[s_], Bmt[l][i_norm]], [BhT[kc][tt]], bias=bias, scale=1.0)
                return go
            return [mk(kc) for kc in range(NKC)]

        def norm_tt(l, i_norm, tt, bk, final=False):
            norm_rstd(tt, bk)
            banks.release(bk)
            if final:
                return
            for op in norm_chunk_ops(l, i_norm, tt):
                op()

        nq = []

        def norm_drip(k):
            for _ in range(min(k, len(nq))):
                nq.pop(0)()

        SWEEP = [(dc, tt) for dc in range(NKC - 2) for tt in range(NTT)] + [(dc, tt) for tt in range(NTT) for dc in (NKC - 2, NKC - 1)]

        SWEEP_W = [(dc, tt) for dc in range(4) for tt in (0, 1)] + [(dc, 2) for dc in range(4)] \
            + [(dc, tt) for tt in range(NTT) for dc in range(4, NKC)]

        def early_norm(dc, tt, nxt, tail0=NKC - 2):
            if nxt is None or dc != tail0 or tt == 0:
                return
            stats_flush(keep=1)
            norm_rstd(tt - 1, stat["banks"][tt - 1])
            banks.release(stat["banks"][tt - 1])
            nq.extend(norm_chunk_ops(nxt[0], nxt[1], tt - 1))

        def finish_norm(nxt):
            if nxt is None:
                return
            stats_flush()
            bk = stat["banks"][NTT - 1]
            stat["banks"] = None
            norm_rstd(NTT - 1, bk)
            banks.release(bk)
            nq.extend(norm_chunk_ops(nxt[0], nxt[1], NTT - 1))

        def ffn_phase(l, which, fence_R, hooks=(), nxt=None):
            hooks = list(hooks)
            gu = din["gu1" if which == 0 else "gu2"][l]
            wd = din["d1" if which == 0 else "d2"][l]
            i_g = 0 if which == 0 else 2
            for grp in range(NJ // 2):
                s = next_A()
                load_A(gu, grp * 256, 256, dst_c0=0, slot=s, halves=(0,))
                load_A(gu, FF + grp * 256, 256, dst_c0=256, slot=s, halves=(1,))
                order = [(jj, tt) for tt in range(NTT) for jj in range(2)] if grp == 0 else [(jj, tt) for jj in range(2) for tt in range(NTT)]
                for (jj, tt) in order:
                    j = grp * 2 + jj
                    if True:
                        bg = banks.get()
                        bu = banks.get()
                        for kc in range(NKC):
                            mm(ps[bg][:], Aslot[s][:, kc, jj * 128:(jj + 1) * 128], hT[:, kc, tt * 512:(tt + 1) * 512], kc == 0, kc == NKC - 1,
                               [BA[s][0], BhT[kc][tt]], [Bps[bg]])
                        for kc in range(NKC):
                            mm(ps[bu][:], Aslot[s][:, kc, 256 + jj * 128:256 + (jj + 1) * 128], hT[:, kc, tt * 512:(tt + 1) * 512], kc == 0, kc == NKC - 1,
                               [BA[s][1], BhT[kc][tt]], [Bps[bu]])
                        sc_ = next_scr()
                        act(scr[:, sc_, :], ps[bg][:], AF.Silu, [Bps[bg]], [Bscr[sc_]])
                        dve_tt(actT[:, j, tt * 512:(tt + 1) * 512], scr[:, sc_, :], ps[bu][:], ALU.mult, [Bscr[sc_], Bps[bu]], [Bact[j][tt]], extra=fence_R)
                        if grp == 0:
                            norm_drip(4)
                if grp == 0:
                    norm_drip(len(nq))
                if hooks:
                    for hk in hooks.pop(0):
                        hk()
            stats_begin("h")
            bslot = {}

            def load_wd(dc):
                if dc not in bslot:
                    s_ = next_B()
                    srcv = wd[:, dc * 128:(dc + 1) * 128].rearrange("(k p) f -> p k f", p=128)
                    dma("pool", Bslot[s_][:, :, :], srcv, [], [BB[s_]], SB[s_], extra=P.fence(all_mix_B_bufs))
                    bslot[dc] = s_
                return bslot[dc]

            for (dc, tt) in SWEEP:
                s = load_wd(dc)
                if dc == NKC - 2:
                    load_wd(NKC - 1)
                c = cond_of_tt(tt)
                b = banks.get()
                for j in range(NJ):
                    mm(ps[b][:], Bslot[s][:, j, :], actT[:, j, tt * 512:(tt + 1) * 512], j == 0, j == NJ - 1, [BB[s], Bact[j][tt]], [Bps[b]])
                stats_flush(keep=1)
                dve_stt(xT[:, dc, tt * 512:(tt + 1) * 512], ps[b][:], gN[:, l, i_g, dc, c:c + 1], xT[:, dc, tt * 512:(tt + 1) * 512], ALU.mult, ALU.add,
                        [Bps[b], Bmg[l][i_g], BxT[dc][tt]], [BxT[dc][tt]])
                stats_update(dc, tt)
                early_norm(dc, tt, nxt)
                if nxt is not None:
                    norm_drip(4)
                if hooks and ((dc < NKC - 2 and tt == NTT - 1) or dc == NKC - 1):
                    for hk in hooks.pop(0):
                        hk()
            while hooks:
                for hk in hooks.pop(0):
                    hk()
            finish_norm(nxt)

        def rope_tile(src_scr, it, dst_bf, reads, writes, extra=()):
            x4 = scr[:, src_scr, :].rearrange("p (g h f) -> p g h f", g=8, h=2)
            s1 = next_scr()
            s2 = next_scr()
            t1 = scr[:, s1, :].rearrange("p (g h f) -> p g h f", g=8, h=2)
            t2 = scr[:, s2, :].rearrange("p (g h f) -> p g h f", g=8, h=2)
            cb = cosT[:, it:it + 1, :].unsqueeze(1).broadcast_to([128, 8, 2, 32])
            sb3 = sinT[:, it:it + 1, :].broadcast_to([128, 8, 32])
            nsb3 = nsinT[:, it:it + 1, :].broadcast_to([128, 8, 32])
            dve_tt(t1, x4, cb, ALU.mult, reads + [Bc["cosT"]], [Bscr[s1]])
            dve_tt(t2[:, :, 0, :], x4[:, :, 1, :], nsb3, ALU.mult, reads + [Bc["nsinT"]], [Bscr[s2]])
            dve_tt(t2[:, :, 1, :], x4[:, :, 0, :], sb3, ALU.mult, reads + [Bc["sinT"]], [Bscr[s2]])
            dve_tt(dst_bf, scr[:, s1, :], scr[:, s2, :], ALU.add, [Bscr[s1], Bscr[s2]], writes, extra=extra)

        mixpre = {}

        def mixer_pre(l, fence_R):
            fr = fence_R
            w_in = din["w_in"][l]
            sGH = load_A(w_in, 1792, 512)
            mixpre["pre"] = {G: load_A(w_in, c0, 512) for G, c0 in (("q", 0), ("k", 512))}

            def conv_pre(tt):
                for cix in range(2):
                    b = banks.get()
                    for kc in range(NKC):
                        mm(ps[b][:], Aslot[sGH][:, kc, cix * 128:(cix + 1) * 128], hT[:, kc, tt * 512:(tt + 1) * 512], kc == 0, kc == NKC - 1,
                           [BA[sGH][0], BhT[kc][tt]], [Bps[b]])
                    act(xc[:, cix, tt * 512:(tt + 1) * 512], ps[b][:], AF.Copy, [Bps[b]], [Bxc[cix][tt]], extra=fr)
                for cix in range(2):
                    b = banks.get()
                    for kc in range(NKC):
                        mm(ps[b][:], Aslot[sGH][:, kc, 256 + cix * 128:256 + (cix + 1) * 128], hT[:, kc, tt * 512:(tt + 1) * 512], kc == 0, kc == NKC - 1,
                           [BA[sGH][1], BhT[kc][tt]], [Bps[b]])
                    dve_tt(xc[:, cix, tt * 512:(tt + 1) * 512], xc[:, cix, tt * 512:(tt + 1) * 512], ps[b][:], ALU.mult, [Bps[b], Bxc[cix][tt]], [Bxc[cix][tt]])
            return conv_pre

        def mixer_phase(l, fence_R, fence_B):
            fr = fence_R
            fb = fence_B
            w_in = din["w_in"][l]
            for pbi in range(2):
                P.op("dve", lambda e, pbi=pbi: e.memset(qpad[pbi][:, :, :], 0.0), reads=[], writes=[Bqpad[pbi]], extra=fb)
            pre = mixpre.pop("pre")
            ckst = Rg[:, 20480:21504].rearrange("p (c h d) -> p c h d", c=2, h=H)
            for c2 in range(2):
                dma("pool", Vs[:, c2, :].rearrange("p (h d) -> p h d", h=H), din["cv"][l, :, c2 * 128:(c2 + 1) * 128, :].rearrange("h s d -> s h d"),
                    [], [BVs[c2]], Scachev[c2], extra=fr)
            for c2 in range(2):
                dma("pool", ckst[:, c2, :, :], din["ck"][l, :, c2 * 128:(c2 + 1) * 128, :].rearrange("h s d -> s h d"), [], [Bckst[c2]], Sck[c2], extra=fr)

            def cache_k_transposes():
                for c2 in range(2):
                    b = banks.get()
                    pb = ps[b][:].bitcast(BF16)
                    for h in range(H):
                        tr(pb[:, h * 128:(h + 1) * 128], ckst[:, c2, h, :], identb[:], [Bckst[c2], Bc["identb"]], [Bps[b]])
                    act(kTs[:, :, c2 * 128:(c2 + 1) * 128], pb[:, 0:512].rearrange("p (h t) -> p h t", h=H), AF.Copy, [Bps[b]], [BkTc], extra=fr)

            if stop_after == "mix_cache":
                return
            pend = []

            def flush_pend(keep=0):
                while len(pend) > keep:
                    pend.pop(0)()

            def tok_transposes(tb, dstT, cols, wbuf):
                def go():
                    b = banks.get()
                    pb = ps[b][:].bitcast(BF16)
                    for h in range(H):
                        tr(pb[:, h * 128:(h + 1) * 128], tokbf[tb][:, h * 128:(h + 1) * 128], identb[:], [Btok[tb], Bc["identb"]], [Bps[b]])
                    dve_copy(dstT[:, :, cols[0]:cols[1]], pb[:, 0:512].rearrange("p (h t) -> p h t", h=H), [Bps[b]], [wbuf], extra=fr)
                return go

            for G, c0 in (("q", 0), ("k", 512), ("v", 1024), ("uvc", 2304)):
                if stop_after == "mix_G" + G:
                    flush_pend()
                    return
                s = pre.pop(G) if G in pre else load_A(w_in, c0, 512)
                for i in range(NTILE):
                    tt = tt_of_tile(i)
                    b = banks.get()
                    for kc in range(NKC):
                        mm(ps[b][:], hT[:, kc, i * 128:(i + 1) * 128], Aslot[s][:, kc, :], kc == 0, kc == NKC - 1, [BhT[kc][tt], BA[s][0], BA[s][1]], [Bps[b]])
                    flush_pend(keep=1)
                    if G == "k" and i == 1:
                        cache_k_transposes()
                    sample = i >= 4
                    it = i - 4
                    if G == "q":
                        tb = state["tok"]
                        state["tok"] = (tb + 1) % 4
                        if sample:
                            s0 = next_scr()
                            act(scr[:, s0, :], ps[b][:], AF.Copy, [Bps[b]], [Bscr[s0]], scale=0.125)
                            rope_tile(s0, it, tokbf[tb], [Bscr[s0]], [Btok[tb]], extra=fb)
                        else:
                            act(tokbf[tb], ps[b][:], AF.Copy, [Bps[b]], [Btok[tb]], scale=0.125, extra=fb)
                        pend.append(tok_transposes(tb, qT, (i * 128, (i + 1) * 128), BqT[i]))
                    elif G == "k":
                        tb = state["tok"]
                        state["tok"] = (tb + 1) % 4
                        s0 = next_scr()
                        act(scr[:, s0, :], ps[b][:], AF.Copy, [Bps[b]], [Bscr[s0]])
                        if sample:
                            rope_tile(s0, it, tokbf[tb], [Bscr[s0]], [Btok[tb]], extra=fb)
                            pend.append(tok_transposes(tb, kTs, (PAST + it * 128, PAST + (it + 1) * 128), BkT[i]))
                        else:
                            seq, ch = i // 2, i % 2
                            dma("sp", dout["nk"][seq, l, :, ch * 128:(ch + 1) * 128, :].rearrange("h s d -> s h d"),
                                scr[:, s0, :].rearrange("p (h d) -> p h d", h=H), [Bscr[s0]], [], Sout[s0])
                            dve_copy(tokbf[tb], scr[:, s0, :], [Bscr[s0]], [Btok[tb]], extra=fb)
                            pend.append(tok_transposes(tb, kTp, (i * 128, (i + 1) * 128), BkT[i]))
                    elif G == "v":
                        if sample:
                            act(Vs[:, 2 + it, :], ps[b][:], AF.Copy, [Bps[b]], [BVs[2 + it]], extra=fr)
                        else:
                            seq, ch = i // 2, i % 2
                            s0 = next_scr()
                            act(scr[:, s0, :], ps[b][:], AF.Copy, [Bps[b]], [Bscr[s0]])
                            dma("sp", dout["nv"][seq, l, :, ch * 128:(ch + 1) * 128, :].rearrange("h s d -> s h d"),
                                scr[:, s0, :].rearrange("p (h d) -> p h d", h=H), [Bscr[s0]], [], Sout[s0])
                            dve_copy(Vp[:, i, :], scr[:, s0, :], [Bscr[s0]], [BVp[i]], extra=fr)
                    else:
                        act(uvbf[:, i, :], ps[b][:], AF.Copy, [Bps[b]], [Bu[i], Bvc[i]] + (Bckst if i < 2 else []), extra=fr)
            flush_pend()
            if stop_after == "mix_z":
                return

            sG3 = load_A(w_in, 1536, 256)
            gb_sb = rstd[:, :, :].rearrange("p a b -> p (a b)").bitcast(BF16).rearrange("p (c t) -> p c t", c=2)
            for cix in range(2):
                for tt in range(NTT):
                    b = banks.get()
                    for kc in range(NKC):
                        mm(ps[b][:], Aslot[sG3][:, kc, cix * 128:(cix + 1) * 128], hT[:, kc, tt * 512:(tt + 1) * 512], kc == 0, kc == NKC - 1,
                           [BA[sG3][0], BhT[kc][tt]], [Bps[b]])
                    act(gb_sb[:, cix, tt * 512:(tt + 1) * 512], ps[b][:], AF.Copy, [Bps[b]], list(Brstd))
            segs = [(0, 0, 256, False, False), (0, 256, 512, False, False), (1, 512, 1024, False, True), (2, 1024, 1536, True, False)]
            conv_jobs = []

            def conv_job(cix, tt, s0_, s1_, lh, rh):
                def go():
                    w0 = convT[:, l, cix, 0:1]
                    w1 = convT[:, l, cix, 1:2]
                    w2 = convT[:, l, cix, 2:3]
                    n = s1_ - s0_
                    sc_ = next_scr()
                    cv = scr[:, sc_, 0:n]
                    xr = [Bxc[cix][t] for t in range(NTT)]
                    dve_ts(cv, xc[:, cix, s0_:s1_], w1, None, ALU.mult, None, xr + [Bc["convT"]], [Bscr[sc_]])
                    lo = 0 if lh else 1
                    dve_stt(scr[:, sc_, lo:n], xc[:, cix, s0_ + lo - 1:s1_ - 1], w0, scr[:, sc_, lo:n], ALU.mult, ALU.add, xr + [Bc["convT"], Bscr[sc_]], [Bscr[sc_]])
                    hi = n if rh else n - 1
                    dve_stt(scr[:, sc_, 0:hi], xc[:, cix, s0_ + 1:s0_ + 1 + hi], w2, scr[:, sc_, 0:hi], ALU.mult, ALU.add, xr + [Bc["convT"], Bscr[sc_]], [Bscr[sc_]])
                    dve_tt(hT[:, 4 + cix, s0_:s1_], cv, gb_sb[:, cix, s0_:s1_], ALU.mult, [Bscr[sc_]] + list(Brstd), [BhT[4 + cix][tt]])
                return go

            for cix in range(2):
                for (tt, s0_, s1_, lh, rh) in segs:
                    conv_jobs.append(conv_job(cix, tt, s0_, s1_, lh, rh))
            if stop_after == "mix_conv":
                while conv_jobs:
                    conv_jobs.pop(0)()
            if stop_after == "mix_conv":
                return

            def chunk_tail(i, b, yb):
                def go():
                    tt = tt_of_tile(i)
                    b2_ = banks.get()
                    pb = ps[b2_][:].bitcast(BF16)
                    for c2 in range(2):
                        tr(pb[:, c2 * 128:(c2 + 1) * 128], ych[yb][:, c2 * 128:(c2 + 1) * 128], identb[:], [Bych[yb], Bc["identb"]], [Bps[b2_]])
                    act(hT[:, 6:8, i * 128:(i + 1) * 128], pb[:, 0:256].rearrange("p (c t) -> p c t", c=2), AF.Copy, [Bps[b2_]], [BhT[6][tt], BhT[7][tt]])
                return go

            for i in range(NTILE):
                b = banks.get()
                for g in range(4):
                    mm(ps[b][:, g * 64:(g + 1) * 64], wsT[:, l, g, :], uvbf[:, i, 256 + g * 64:256 + (g + 1) * 64], True, True, [Bc["wsT"], Bvc[i]], [Bps[b]])
                flush_pend()
                yb = i % 2
                s_ = next_scr()
                dve_tt(scr[:, s_, 0:256].rearrange("p (g c) -> p g c", g=4), ps[b][:, 0:256].rearrange("p (g c) -> p g c", g=4),
                       cbT[:, l, :].unsqueeze(2).broadcast_to([128, 4, 64]), ALU.add, [Bps[b], Bc["cbT"]], [Bscr[s_]])
                dve_tt(ych[yb][:, 0:256], scr[:, s_, 0:256], uvbf[:, i, 0:256], ALU.mult, [Bscr[s_], Bu[i]], [Bych[yb]], extra=fb)
                pend.append(chunk_tail(i, b, yb))
            flush_pend()
            if stop_after == "mix_chunk":
                return

            nlam = lamS[:, 4 + l:5 + l]
            LA = 3
            blocks = []
            for seq in range(2):
                for h in range(H):
                    blocks.append(dict(h=h, qcols=seq * 256, nq=256, kT=kTp, kcols=seq * 256, Vt=Vp[:, 2 * seq:2 * seq + 2, :],
                                       Vbufs=[BVp[2 * seq], BVp[2 * seq + 1]], kbufs=[BkT[2 * seq], BkT[2 * seq + 1]],
                                       qbufs=[BqT[2 * seq], BqT[2 * seq + 1]], nkc=2, tts=[0]))
            for qt in range(2):
                for h in range(H):
                    blocks.append(dict(h=h, qcols=512 + qt * 512, nq=512, kT=kTs, kcols=0, Vt=Vs, Vbufs=BVs, kbufs=[BkTc] + BkT[4:12],
                                       qbufs=BqT[4 + 4 * qt:8 + 4 * qt], nkc=10, tts=[1 + qt]))
            steps = [(bi, m, kc) for bi, blk in enumerate(blocks) for m in range(2) for kc in range(blk["nkc"])]
            sbank = {}
            ebuf = {}
            accb = {}
            later = []

            def prep(bi):
                blk = blocks[bi]
                pbi = bi % 2
                nq, h, qc = blk["nq"], blk["h"], blk["qcols"]
                dve_copy(qpad[pbi][0:64, 0, 0:nq], qT[0:64, h, qc:qc + nq], blk["qbufs"], [Bqpad[pbi]])
                dve_copy(qpad[pbi][64:128, 1, 0:nq], qT[64:128, h, qc:qc + nq], blk["qbufs"], [Bqpad[pbi]])

            def emit_S(t):
                bi, m, kc = steps[t]
                blk = blocks[bi]
                pbi = bi % 2
                nq, h = blk["nq"], blk["h"]
                b = state["sb"]
                state["sb"] = (b + 1) % 4
                sbank[t] = b
                mm(ps[b][:, 0:nq], blk["kT"][:, h, blk["kcols"] + kc * 128:blk["kcols"] + (kc + 1) * 128], qpad[pbi][:, m, 0:nq], True, True,
                   blk["kbufs"] + [Bqpad[pbi]], [Bps[b]])
                ei = state["E"]
                state["E"] = (ei + 1) % 4
                ebuf[t] = ei
                act(Ebuf[ei][:, 0:nq], ps[b][:, 0:nq], AF.Exp, [Bps[b]], [BE[ei]], extra=fb)

            def abuf(bi):
                nq = blocks[bi]["nq"]
                if nq == 512:
                    k = bi % 2
                    return asc_all[:, k * 512:(k + 1) * 512], sqo_all[:, k * 512:(k + 1) * 512], Bascq[2 * k:2 * k + 2], Bsqo[2 * k:2 * k + 2]
                k = bi % 4
                return asc_all[:, k * 256:(k + 1) * 256], sqo_all[:, k * 256:(k + 1) * 256], [Bascq[k]], [Bsqo[k]]

            def evac(bi, m):
                blk = blocks[bi]
                nq = blk["nq"]
                bo, bs_ = accb[(bi, m)]
                a0, sq_, ab_, sb_ = abuf(bi)
                si = next_scr()
                if nq == 256 or m == 0:
                    act(scr[:, si, 0:nq], ps[bs_][:, 0:nq], AF.Ln, [Bps[bs_]], [Bscr[si]])
                    act(scr[:, si, 0:nq], scr[:, si, 0:nq], AF.Exp, [Bscr[si]], [Bscr[si]], scale=-1.0)
                else:
                    dve_copy(scr[:, si, 0:nq], ps[bs_][:, 0:nq], [Bps[bs_]], [Bscr[si]])
                    dve_recip(scr[:, si, 0:nq], scr[:, si, 0:nq], [Bscr[si]], [Bscr[si]])
                if m == 0:
                    dve_tt(a0, ps[bo][:, 0:nq], scr[:, si, 0:nq], ALU.mult, [Bps[bo], Bscr[si]], ab_, extra=fb)
                else:
                    dve_tt(scr[:, si, 0:nq], ps[bo][:, 0:nq], scr[:, si, 0:nq], ALU.mult, [Bps[bo], Bscr[si]], [Bscr[si]])
                    dve_stt(a0, scr[:, si, 0:nq], nlam, a0, ALU.mult, ALU.add, [Bscr[si], Bc["lamS"]] + ab_, ab_)
                    dve_tt(sq_, a0, a0, ALU.mult, ab_, sb_, extra=fr)

            def finalize(bi, t):
                blk = blocks[bi]
                nq, h, qc = blk["nq"], blk["h"], blk["qcols"]
                a0, sq_, ab_, sb_ = abuf(bi)

                def pe_part():
                    bn = state["sb"]
                    state["sb"] = (bn + 1) % 4
                    mm(ps[bn][:, 0:nq], onesb[:], sq_, True, True, [Bc["onesb"]] + sb_, [Bps[bn]])
                    si = next_scr()
                    act(scr[:, si, 0:nq], ps[bn][:, 0:nq], AF.Ln, [Bps[bn]], [Bscr[si]], bias=EPS, scale=1.0 / 128.0)
                    act(scr[:, si, 0:nq], scr[:, si, 0:nq], AF.Exp, [Bscr[si]], [Bscr[si]], scale=-0.5)
                    dve_stt(hT[:, h, qc:qc + nq], a0, gsub[:, l, h:h + 1], scr[:, si, 0:nq], ALU.mult, ALU.mult,
                            ab_ + [Bc["gsub"], Bscr[si]], [BhT[h][t_] for t_ in blk["tts"]])
                later.append((t + (14 if nq == 512 else 6), pe_part))

            def emit_PV(t):
                bi, m, kc = steps[t]
                blk = blocks[bi]
                nq, h, nkc = blk["nq"], blk["h"], blk["nkc"]
                if kc == 0 and m == 1 and nkc == 2 and bi + 2 < len(blocks):
                    prep(bi + 2)
                if kc == 0 and m == 0 and bi + 1 < len(blocks) and bi >= 1 and blocks[bi - 1]["nkc"] != 2:
                    prep(bi + 1)
                if kc == 0 and m == 0 and nq == 512 and conv_jobs:
                    conv_jobs.pop(0)()
                if kc == 0:
                    aset = state["aset"]
                    state["aset"] = 1 - aset
                    accb[(bi, m)] = (4 + 2 * aset, 5 + 2 * aset)
                bo, bs_ = accb[(bi, m)]
                ei = ebuf[t]
                mm(ps[bo][:, 0:nq], blk["Vt"][:, kc, h * 128:(h + 1) * 128], Ebuf[ei][:, 0:nq], kc == 0, kc == nkc - 1, [blk["Vbufs"][kc], BE[ei]], [Bps[bo]])
                mm(ps[bs_][:, 0:nq], onesb[:], Ebuf[ei][:, 0:nq], kc == 0, kc == nkc - 1, [Bc["onesb"], BE[ei]], [Bps[bs_]])
                if kc == nkc - 1:
                    evac(bi, m)
                    if m == 1:
                        finalize(bi, t)

            nst = len(steps)
            prep(0)
            prep(1)
            for t in range(min(LA, nst)):
                emit_S(t)
            for t in range(nst):
                if t + LA < nst:
                    emit_S(t + LA)
                while later and later[0][0] <= t:
                    later.pop(0)[1]()
                emit_PV(t)
            while conv_jobs:
                conv_jobs.pop(0)()
            banks.rr = 4

            if stop_after == "mix_attn":
                while later:
                    later.pop(0)[1]()
                return
            stats_begin("q")
            aslot = {}
            for gi, (dc, tt) in enumerate(SWEEP_W):
                half, dq = dc // 4, dc % 4
                if half not in aslot:
                    aslot[half] = load_A(din["w_out"][l], half * 512, 512)
                s = aslot[half]
                c = cond_of_tt(tt)
                b = banks.get()
                for kc in range(NKC):
                    mm(ps[b][:], Aslot[s][:, kc, dq * 128:(dq + 1) * 128], hT[:, kc, tt * 512:(tt + 1) * 512], kc == 0, kc == NKC - 1,
                       [BA[s][dq // 2], BhT[kc][tt]], [Bps[b]])
                stats_flush(keep=1)
                dve_stt(xT[:, dc, tt * 512:(tt + 1) * 512], ps[b][:], gN[:, l, 1, dc, c:c + 1], xT[:, dc, tt * 512:(tt + 1) * 512], ALU.mult, ALU.add,
                        [Bps[b], Bmg[l][1], BxT[dc][tt]], [BxT[dc][tt]])
                stats_update(dc, tt)
                early_norm(dc, tt, (l, 2), tail0=4)
                norm_drip(2)
                if gi == 3:
                    while later:
                        later.pop(0)[1]()
            finish_norm((l, 2))

        def final_phase():
            fr = P.fence(all_act_bufs + all_mix_R_bufs)
            outT = [Rg[:, k * 8192:(k + 1) * 8192].bitcast(F32).rearrange("p (k t) -> p k t", k=NKC) for k in range(2)]
            ost = [Rg[:, 16384 + k * 2048:16384 + (k + 1) * 2048].bitcast(F32) for k in range(8)]
            Bout = [[Buf(f"outT{o}_{k}") for k in range(NKC)] for o in range(2)]
            Bost = [[Buf(f"ost{k}_{h}") for h in range(2)] for k in range(8)]
            norm_phase(0, 0, final=True)
            for tt in range(NTT):
                o = tt % 2
                for kc in range(NKC):
                    dve_stt(outT[o][:, kc, :], xT[:, kc, tt * 512:(tt + 1) * 512], fnormT[:, kc:kc + 1], rstd[:, tt, :], ALU.mult, ALU.mult,
                            [BxT[kc][tt], Bc["fnormT"], Brstd[tt]], [Bout[o][kc]], extra=fr)
                for q in range(4):
                    i = tt * 4 + q
                    k8 = i % 8
                    for half in range(2):
                        b = banks.get()
                        for k4 in range(4):
                            kc = half * 4 + k4
                            tr(ps[b][:, k4 * 128:(k4 + 1) * 128], outT[o][:, kc, q * 128:(q + 1) * 128], ident[:], [Bout[o][kc], Bc["ident"]], [Bps[b]])
                        if half == 0:
                            act(ost[k8][:, 0:512], ps[b][:], AF.Copy, [Bps[b]], [Bost[k8][0]], extra=fr)
                        else:
                            act(ost[k8][:, 512:1024], ps[b][:], AF.Copy, [Bps[b]], [Bost[k8][1]], extra=fr)
                    dst = dout["yp"][i * 128:(i + 1) * 128, :] if i < 4 else dout["ys"][(i - 4) * 128:(i - 3) * 128, :]
                    dma("sp", dst, ost[k8], Bost[k8], [], Sost[k8])

        def grp_hook(l, gqs, fins=()):
            return [(lambda l=l, gq=gq: mod_group(l, gq)) for gq in gqs] + [(lambda l=l, i=i: mod_finalize(l, *i) if isinstance(i, tuple) else mod_finalize(l, i)) for i in fins]

        x_tiles(range(0, 8))
        mod_group(0, 0, extra=[xops[1]])
        x_tiles(range(8, 12))
        stat["banks"] = [banks.get(hold=True) for _ in range(NTT)]
        stat["pend"] = []
        for tt in range(NTT):
            norm_stats(tt, stat["banks"][tt])
        for gq in range(1, 4):
            mod_group(0, gq)
        mod_finalize(0, 0, "a")
        dump("xT0", xT[:], [b for row in BxT for b in row])
        for l in range(L):
            if l == 0:
                norm_phase(l, 0, drip_tail=True)
            if l == 0:
                dump("h1", hT[:], [b for row in BhT for b in row])
                hooks1 = [grp_hook(0, [4]), grp_hook(0, [5], fins=[(0, "g")])] + [grp_hook(0, [6 + k]) for k in range(5)] + [grp_hook(0, [11], fins=[1])] \
                    + [grp_hook(0, [12 + k]) for k in range(5)] + [grp_hook(0, [17], fins=[2])]
            else:
                hooks1 = []
            ffn_phase(l, 0, P.fence(all_mix_R_bufs), hooks=hooks1, nxt=(l, 1))
            if l == 0:
                dump("x1", xT[:], [b for row in BxT for b in row])
            if stop_after == "ffn1":
                break
            if l == 0:
                lam_setup()
            fR = P.fence(all_act_bufs)
            conv_pre = mixer_pre(l, fR)
            conv_pre(0)
            norm_drip(len(nq))
            conv_pre(1)
            conv_pre(2)
            mixer_phase(l, fR, P.fence(BB))
            if l == 0:
                dump("x2", xT[:], [b for row in BxT for b in row])
            if stop_after is not None and stop_after.startswith("mix"):
                break
            if l + 1 < L:
                hooks2 = [grp_hook(l + 1, [k], fins=([0] if k == 5 else [1] if k == 11 else [2] if k == 17 else [])) for k in range(18)]
            else:
                hooks2 = []
            ffn_phase(l, 1, P.fence(all_mix_R_bufs), hooks=hooks2, nxt=((l + 1, 0) if l + 1 < L else None))
            if l == 0:
                dump("x3", xT[:], [b for row in BxT for b in row])
        final_phase()
        counts = P.emit(nc, es, final_waits=out_sems)
    return nc, counts, list(ddbg.keys())


def _rope_tables():
    n_rows = TS // 64
    row = np.repeat(np.arange(n_rows, dtype=np.float32), 64)
    col = np.tile(np.arange(64, dtype=np.float32), n_rows)
    inv = (np.float32(10000.0) ** (-np.arange(16, dtype=np.float32) / np.float32(16))).astype(np.float32)
    ang = np.concatenate([row[:, None] * inv, col[:, None] * inv], axis=-1).astype(np.float32)
    cos = np.cos(ang).astype(np.float32)
    sin = np.sin(ang).astype(np.float32)

    def tm(a):
        return np.ascontiguousarray(a.reshape(8, 128, 32).transpose(1, 0, 2))
    return tm(cos), tm(sin), tm(-sin)


def make_in_maps(x_prompt, x_sample, cache_k, cache_v, c, c_ctx, w_mod, b_mod, norm_g,
                 ffn1_w_gu, ffn1_w_d, ffn2_w_gu, ffn2_w_d, w_in, w_out, attn_lam,
                 attn_subln_g, conv_w, chunk_ws, chunk_b, final_norm_g):
    f = lambda a: np.ascontiguousarray(np.asarray(a, dtype=np.float32))
    cosT, sinT, nsinT = _rope_tables()
    shared = {
        "w_mod": f(w_mod), "gu1": f(ffn1_w_gu), "d1": f(ffn1_w_d), "gu2": f(ffn2_w_gu), "d2": f(ffn2_w_d),
        "w_in": f(w_in), "w_out": f(w_out),
        "bmodT": f(np.asarray(b_mod).reshape(L, 72, 128).transpose(2, 0, 1)),
        "normgT": f(np.asarray(norm_g).reshape(L, 3, NKC, 128).transpose(3, 0, 1, 2)),
        "fnormT": f(np.asarray(final_norm_g).reshape(NKC, 128).transpose(1, 0)),
        "lamA": f(np.broadcast_to(np.asarray(attn_lam)[:, 0::2, :].reshape(1, L * 2, 64), (128, L * 2, 64))),
        "lamB": f(np.broadcast_to(np.asarray(attn_lam)[:, 1::2, :].reshape(1, L * 2, 64), (128, L * 2, 64))),
        "sublnT": f(np.asarray(attn_subln_g).transpose(2, 0, 1)),
        "convT": f(np.asarray(conv_w).reshape(L, 3, 2, 128).transpose(3, 0, 2, 1)),
        "wsT": f(np.asarray(chunk_ws).transpose(3, 0, 1, 2)),
        "cbT": f(np.asarray(chunk_b).transpose(2, 0, 1)),
        "ident": np.eye(128, dtype=np.float32), "ones": np.ones((128, 128), dtype=np.float32),
        "cosT": cosT, "sinT": sinT, "nsinT": nsinT,
    }
    x_prompt = np.asarray(x_prompt, dtype=np.float32)
    x_sample = np.asarray(x_sample, dtype=np.float32)
    cache_k = np.asarray(cache_k, dtype=np.float32)
    cache_v = np.asarray(cache_v, dtype=np.float32)
    c = np.asarray(c, dtype=np.float32)
    c_ctx = np.asarray(c_ctx, dtype=np.float32)
    maps = []
    for cid in range(NCORES):
        cond = np.stack([c_ctx, c[cid]], axis=0)
        m = dict(shared)
        m["xp"] = f(x_prompt[2 * cid:2 * cid + 2].reshape(TP, D))
        m["xs"] = f(x_sample[cid])
        m["ck"] = f(cache_k[cid])
        m["cv"] = f(cache_v[cid])
        m["condT"] = f(cond.reshape(2, NKC, 128).transpose(2, 1, 0))
        maps.append(m)
    return maps


_CACHE = {}


def kernel(**inputs):
    if "nc" not in _CACHE:
        _CACHE["nc"] = build_program()[0]
    nc = _CACHE["nc"]
    maps = make_in_maps(**inputs)
    res = run_bass_kernel_spmd(nc, maps, core_ids=list(range(NCORES)))
    rs = res.results
    y_prompt = np.concatenate([r["yp"].reshape(2, 256, D) for r in rs], axis=0).astype(np.float32)
    y_sample = np.stack([r["ys"] for r in rs], axis=0).astype(np.float32)
    new_k = np.concatenate([r["nk"] for r in rs], axis=0).astype(np.float32)
    new_v = np.concatenate([r["nv"] for r in rs], axis=0).astype(np.float32)
    return (y_prompt, y_sample, new_k, new_v)
```

```python
import contextlib
import math
import numpy as np
import concourse.bass as bass
import concourse.mybir as mybir
from concourse.bass_utils import run_bass_kernel_spmd

F32 = mybir.dt.float32
BF16 = mybir.dt.bfloat16
AF = mybir.ActivationFunctionType
ALU = mybir.AluOpType
AX = mybir.AxisListType

NCORES = 8
D = 1024
FF = 2816
L = 2
H = 4
TP = 512
TS = 1024
T = TP + TS
PAST = 256
NKC = 8
NJ = 22
NTILE = 12
NTT = 3
EPS = 1e-6


class Buf:
    __slots__ = ("name", "w", "r")

    def __init__(self, name):
        self.name = name
        self.w = None
        self.r = []


class DmaSem:
    __slots__ = ("name", "count", "handle", "group")

    def __init__(self, name, group=False):
        self.name = name
        self.count = 0
        self.handle = None
        self.group = group


class Op:
    __slots__ = ("eng", "fn", "deps", "signal", "val", "dma", "tag")

    def __init__(self, eng, fn, dma, tag):
        self.eng = eng
        self.fn = fn
        self.deps = []
        self.signal = False
        self.val = 0
        self.dma = dma
        self.tag = tag


ENGS = ("pe", "act", "dve", "pool", "sp")
EPOCH = 30000


class Prog:
    def __init__(self, same_engine_sync=True):
        self.ops = {e: [] for e in ENGS}
        self.dmasems = []
        self.same_engine_sync = same_engine_sync
        self.all_ops = []

    def dmasem(self, name, group=False):
        s = DmaSem(name, group)
        self.dmasems.append(s)
        return s

    def fence(self, bufs):
        deps = {}
        for b in bufs:
            if b.w is not None:
                deps[id(b.w)] = b.w
            for r in b.r:
                deps[id(r)] = r
        return list(deps.values())

    def op(self, eng, fn, reads=(), writes=(), dma=None, tag="", extra=()):
        o = Op(eng, fn, dma, tag)
        deps = {}
        for b in reads:
            if b.w is not None:
                deps[id(b.w)] = b.w
        for b in writes:
            if b.w is not None:
                deps[id(b.w)] = b.w
            for r in b.r:
                deps[id(r)] = r
        for d in extra:
            deps[id(d)] = d
        o.deps = list(deps.values())
        for b in reads:
            if dma is None:
                b.r = [r for r in b.r if not (r.dma is None and r.eng == eng)]
            b.r.append(o)
        for b in writes:
            b.w = o
            b.r = []
        if dma is not None:
            dma.count += 16
            o.val = dma.count
            o.signal = True
        self.ops[eng].append(o)
        self.all_ops.append(o)
        return o

    def _needs_wait(self, o, d):
        if d.dma is not None:
            return True
        if d.eng != o.eng:
            return True
        if o.dma is not None:
            return True
        if d.eng == "pe":
            return False
        return self.same_engine_sync

    def emit(self, nc, es, final_waits=()):
        for o in self.all_ops:
            for d in o.deps:
                if self._needs_wait(o, d):
                    d.signal = True
        counts = {}
        for e in ENGS:
            c = 0
            for o in self.ops[e]:
                if o.dma is None and o.signal:
                    c += 1
                    o.val = c
            counts[e] = c
        engsems = {}
        for e in ENGS:
            n_ep = (counts[e] + EPOCH - 1) // EPOCH
            engsems[e] = [es.enter_context(nc.semaphore(f"c_{e}_{i}")) for i in range(n_ep)]
        for s in self.dmasems:
            if s.count > 0:
                s.handle = es.enter_context(nc.semaphore(f"d_{s.name}"))
        block = es.enter_context(nc.Block())

        def semval(d):
            if d.dma is not None:
                return d.dma.handle, (d.dma.count if d.dma.group else d.val)
            ep = (d.val - 1) // EPOCH
            return engsems[d.eng][ep], d.val - ep * EPOCH

        def run(ename, engine):
            waited = {}
            for o in self.ops[ename]:
                for d in o.deps:
                    if not self._needs_wait(o, d):
                        continue
                    sem, val = semval(d)
                    k = sem.num
                    if waited.get(k, 0) >= val:
                        continue
                    engine.wait_ge(sem, val)
                    waited[k] = val
                ins = o.fn(engine)
                if o.signal:
                    if o.dma is not None:
                        ins.then_inc(o.dma.handle, 16)
                    else:
                        ep = (o.val - 1) // EPOCH
                        ins.then_inc(engsems[ename][ep], 1)
            if ename == "sp":
                for s in final_waits:
                    if s.count > 0:
                        engine.wait_ge(s.handle, s.count)

        block.sync(lambda eng: run("sp", eng))
        block.tensor(lambda eng: run("pe", eng))
        block.scalar(lambda eng: run("act", eng))
        block.vector(lambda eng: run("dve", eng))
        block.gpsimd(lambda eng: run("pool", eng))
        return counts


IN_SPECS = [
    ("xp", [TP, D]), ("xs", [TS, D]), ("ck", [L, H, PAST, 128]), ("cv", [L, H, PAST, 128]),
    ("condT", [128, NKC, 2]),
    ("w_mod", [L, D, 9 * D]), ("gu1", [L, D, 2 * FF]), ("d1", [L, FF, D]),
    ("gu2", [L, D, 2 * FF]), ("d2", [L, FF, D]), ("w_in", [L, D, FF]), ("w_out", [L, D, D]),
    ("bmodT", [128, L, 72]), ("normgT", [128, L, 3, NKC]), ("fnormT", [128, NKC]),
    ("lamA", [128, L * 2, 64]), ("lamB", [128, L * 2, 64]), ("sublnT", [128, L, H]),
    ("convT", [128, L, 2, 3]), ("wsT", [128, L, 4, 128]), ("cbT", [128, L, 4]),
    ("ident", [128, 128]), ("ones", [128, 128]),
    ("cosT", [128, 8, 32]), ("sinT", [128, 8, 32]), ("nsinT", [128, 8, 32]),
]
OUT_SPECS = [("yp", [TP, D]), ("ys", [TS, D]), ("nk", [2, L, H, 256, 128]), ("nv", [2, L, H, 256, 128])]


def build_program(dbg_names=(), stop_after=None):
    nc = bass.Bass("TRN2", target_bir_lowering=False)
    din = {n: nc.dram_tensor(n, s, F32, kind="ExternalInput").ap() for n, s in IN_SPECS}
    dout = {n: nc.dram_tensor(n, s, F32, kind="ExternalOutput").ap() for n, s in OUT_SPECS}
    ddbg = {}
    P = Prog()
    es = contextlib.ExitStack()
    with es:
        def sb(name, shape, dt):
            return es.enter_context(nc.sbuf_tensor("s_" + name, shape, dt))

        xT = sb("xT", [128, NKC, T], F32)
        hT = sb("hT", [128, NKC, T], BF16)
        Rg = sb("Rg", [128, NJ * T], BF16)
        Aslot = [sb(f"A{i}", [128, NKC, 512], BF16) for i in range(3)]
        Bslot = [sb(f"Bw{i}", [128, NJ, 128], BF16) for i in range(3)]
        scr = sb("scr", [128, 4, 512], F32)
        rstd = sb("rstd", [128, NTT, 512], F32)
        ident = sb("ident", [128, 128], F32)
        identb = sb("identb", [128, 128], BF16)
        onesb = sb("onesb", [128, 128], BF16)
        condT = sb("condT", [128, NKC, 2], F32)
        scondT = sb("scondT", [128, NKC, 2], BF16)
        mT = sb("mT", [128, L, 72, 2], F32)
        aN = sb("aN", [128, L, 3, NKC, 2], F32)
        gN = sb("gN", [128, L, 3, NKC, 2], F32)
        bmodT = sb("bmodT", [128, L, 72], F32)
        normgT = sb("normgT", [128, L, 3, NKC], F32)
        fnormT = sb("fnormT", [128, NKC], F32)
        lamA = sb("lamA", [128, L * 2, 64], F32)
        lamB = sb("lamB", [128, L * 2, 64], F32)
        lamS = sb("lamS", [128, 8], F32)
        sublnT = sb("sublnT", [128, L, H], F32)
        gsub = sb("gsub", [128, L, H], F32)
        convT = sb("convT", [128, L, 2, 3], F32)
        wsT = sb("wsT", [128, L, 4, 128], BF16)
        cbT = sb("cbT", [128, L, 4], F32)
        cosT = sb("cosT", [128, 8, 32], F32)
        sinT = sb("sinT", [128, 8, 32], F32)
        nsinT = sb("nsinT", [128, 8, 32], F32)
        ps = [es.enter_context(nc.psum_tensor(f"ps{i}", [128, 512], F32)) for i in range(8)]

        actT = Rg[:, :].rearrange("p (j t) -> p j t", j=NJ)
        qT = Rg[:, 0:6144].rearrange("p (h t) -> p h t", h=H)
        kTs = Rg[:, 6144:11264].rearrange("p (h t) -> p h t", h=H)
        kTp = Rg[:, 11264:13312].rearrange("p (h t) -> p h t", h=H)
        Vs = Rg[:, 13312:18432].rearrange("p (c f) -> p c f", c=10)
        Vp = Rg[:, 18432:20480].rearrange("p (c f) -> p c f", c=4)
        uvbf = Rg[:, 20480:26624].rearrange("p (c f) -> p c f", c=NTILE)
        xc = Rg[:, 26624:32768].bitcast(F32).rearrange("p (c t) -> p c t", c=2)
        b0 = Bslot[0][:, :, :].rearrange("p a b -> p (a b)")
        b1 = Bslot[1][:, :, :].rearrange("p a b -> p (a b)")
        b2 = Bslot[2][:, :, :].rearrange("p a b -> p (a b)")
        qpad = [b0[:, i * 1024:(i + 1) * 1024].rearrange("p (m t) -> p m t", m=2) for i in range(2)]
        ych = [b0[:, 2048 + i * 256:2048 + (i + 1) * 256] for i in range(2)]
        Ebuf = [b1[:, i * 512:(i + 1) * 512] for i in range(4)]
        tokbf = Ebuf
        sqo_all = Rg[:, 32768:33792]
        asc_all = b2[:, 0:2048].bitcast(F32)
        asc = [b2[:, i * 1024:(i + 1) * 1024].bitcast(F32) for i in range(2)]
        xin = [scr[:, 0:2, :].rearrange("p a b -> p (a b)"), scr[:, 2:4, :].rearrange("p a b -> p (a b)")]

        BxT = [[Buf(f"xT{k}_{t}") for t in range(NTT)] for k in range(NKC)]
        BhT = [[Buf(f"hT{k}_{t}") for t in range(NTT)] for k in range(NKC)]
        Bact = [[Buf(f"act{j}_{t}") for t in range(NTT)] for j in range(NJ)]
        BA = [[Buf(f"A{i}_{h}") for h in range(2)] for i in range(3)]
        BB = [Buf(f"Bw{i}") for i in range(3)]
        Bscr = [Buf(f"scr{i}") for i in range(4)]
        Brstd = [Buf(f"rstd{i}") for i in range(NTT)]
        Bps = [Buf(f"ps{i}") for i in range(8)]
        Bc = {n: Buf(n) for n in ("ident", "identb", "onesb", "condT", "scondT", "mT", "aN", "gN", "bmodT", "normgT",
                                  "fnormT", "lamA", "lamB", "lamS", "sublnT", "gsub", "convT", "wsT", "cbT",
                                  "cosT", "sinT", "nsinT")}
        BqT = [Buf(f"qT{i}") for i in range(NTILE)]
        BkT = [Buf(f"kT{i}") for i in range(NTILE)]
        BkTc = Buf("kTc")
        Bckst = [Buf(f"ckst{i}") for i in range(2)]
        Bxin = [Buf(f"xin{i}") for i in range(3)]
        Bmsp = [Buf(f"modsp{i}") for i in range(2)]
        BVs = [Buf(f"Vs{i}") for i in range(10)]
        BVp = [Buf(f"Vp{i}") for i in range(4)]
        Bu = [Buf(f"u{i}") for i in range(NTILE)]
        Bvc = [Buf(f"vc{i}") for i in range(NTILE)]
        Bxc = [[Buf(f"xc{c}_{t}") for t in range(NTT)] for c in range(2)]
        Bqpad = [Buf(f"qpad{i}") for i in range(2)]
        Bych = [Buf(f"ych{i}") for i in range(2)]
        BE = [Buf(f"E{i}") for i in range(4)]
        Btok = BE
        Bsqo = [Buf(f"sqo{i}") for i in range(4)]
        Bascq = [Buf(f"ascq{i}") for i in range(4)]
        Basc = [Buf(f"asc{i}") for i in range(2)]
        all_act_bufs = [b for row in Bact for b in row]
        all_mix_R_bufs = Bxin + Bmsp + Bckst + BqT + BkT + [BkTc] + BVs + BVp + Bu + Bvc + [b for row in Bxc for b in row] + Bsqo
        all_mix_B_bufs = Bqpad + Bych + BE + Basc + Bascq

        SA = [[P.dmasem(f"A{i}_{h}") for h in range(2)] for i in range(3)]
        SB = [P.dmasem(f"B{i}") for i in range(3)]
        Sconst = P.dmasem("const", group=True)
        Sconst0 = P.dmasem("const0", group=True)
        Sx = [P.dmasem(f"x{i}") for i in range(3)]
        Smsp = [P.dmasem(f"modsp{i}") for i in range(2)]
        Sost = [P.dmasem(f"ost{i}") for i in range(8)]
        Sconstp = P.dmasem("constp", group=True)
        Sscr = [P.dmasem(f"scr{i}") for i in range(4)]
        Sout = [P.dmasem(f"out{i}") for i in range(4)]
        Scachev = [P.dmasem(f"cachev{i}") for i in range(2)]
        Sck = [P.dmasem(f"cachek{i}") for i in range(2)]
        out_sems = list(Sout) + list(Sost)

        class Banks:
            def __init__(self):
                self.held = set()
                self.rr = 0

            def get(self, hold=False):
                for _ in range(16):
                    b = self.rr
                    self.rr = (self.rr + 1) % 8
                    if b not in self.held:
                        if hold:
                            self.held.add(b)
                        return b
                raise RuntimeError("no psum bank")

            def release(self, b):
                self.held.discard(b)

        banks = Banks()
        state = {"A": 0, "B": 0, "scr": 0, "E": 0, "tok": 0, "alt": 0, "sb": 0, "aset": 0}

        def next_A():
            i = state["A"]
            state["A"] = (i + 1) % 3
            return i

        def next_B():
            i = state["B"]
            state["B"] = (i + 1) % 3
            return i

        def next_scr():
            i = state["scr"]
            state["scr"] = (i + 1) % 4
            return i

        def tt_of_tile(i):
            return i // 4

        def cond_of_tt(tt):
            return 0 if tt == 0 else 1

        def mm(out, lhsT, rhs, start, stop, reads, writes, extra=()):
            P.op("pe", lambda e: e.matmul(out, lhsT, rhs, start=start, stop=stop), reads=reads, writes=writes, extra=extra)

        def tr(out, in_, idt, reads, writes, extra=()):
            P.op("pe", lambda e: e.transpose(out, in_, idt), reads=reads, writes=writes, extra=extra)

        def act(out, in_, func, reads, writes, bias=None, scale=None, extra=()):
            kw = {}
            if bias is not None:
                kw["bias"] = bias
            if scale is not None:
                kw["scale"] = scale
            P.op("act", lambda e: e.activation(out=out, in_=in_, func=func, **kw), reads=reads, writes=writes, extra=extra)

        def dve_tt(out, in0, in1, op, reads, writes, extra=()):
            P.op("dve", lambda e: e.tensor_tensor(out=out, in0=in0, in1=in1, op=op), reads=reads, writes=writes, extra=extra)

        def dve_ts(out, in0, s1, s2, op0, op1, reads, writes, extra=()):
            if s2 is None:
                P.op("dve", lambda e: e.tensor_scalar(out=out, in0=in0, scalar1=s1, scalar2=None, op0=op0), reads=reads, writes=writes, extra=extra)
            else:
                P.op("dve", lambda e: e.tensor_scalar(out=out, in0=in0, scalar1=s1, scalar2=s2, op0=op0, op1=op1), reads=reads, writes=writes, extra=extra)

        def dve_stt(out, in0, scalar, in1, op0, op1, reads, writes, extra=()):
            P.op("dve", lambda e: e.scalar_tensor_tensor(out=out, in0=in0, scalar=scalar, in1=in1, op0=op0, op1=op1), reads=reads, writes=writes, extra=extra)

        def dve_copy(out, in_, reads, writes, extra=()):
            P.op("dve", lambda e: e.tensor_copy(out=out, in_=in_), reads=reads, writes=writes, extra=extra)

        def dve_recip(out, in_, reads, writes):
            P.op("dve", lambda e: e.reciprocal(out=out, in_=in_), reads=reads, writes=writes)

        def dma(q, out, in_, reads, writes, sem, extra=()):
            return P.op(q, lambda e: e.dma_start(out=out, in_=in_), reads=reads, writes=writes, dma=sem, extra=extra)

        dbg_n = [0]

        def dump(name, ap, reads):
            if name not in dbg_names:
                return
            t = nc.dram_tensor("dbg_" + name, list(ap.shape), ap.dtype, kind="ExternalOutput").ap()
            ddbg[name] = t
            s = P.dmasem(f"dbg{dbg_n[0]}")
            dbg_n[0] += 1
            out_sems.append(s)
            dma("sp", t, ap, reads, [], s)

        for n, t in (("ident", ident), ("condT", condT)):
            dma("sp", t[:], din[n], [], [Bc[n]], Sconst0)
        for n, t in (("bmodT", bmodT), ("normgT", normgT), ("fnormT", fnormT),
                     ("lamA", lamA), ("lamB", lamB), ("sublnT", sublnT), ("convT", convT), ("cbT", cbT),
                     ("cosT", cosT), ("sinT", sinT), ("nsinT", nsinT)):
            dma("act", t[:], din[n], [], [Bc[n]], Sconst)
        xops = []
        xin_all = Rg[:, 0:24576].bitcast(F32).rearrange("p (n d) -> p n d", n=NTILE)
        for g in range(3):
            srcx = din["xp"] if g == 0 else din["xs"][(g - 1) * 512:g * 512, :]
            xops.append(dma("sp", xin_all[:, 4 * g:4 * g + 4, :], srcx.rearrange("(n p) d -> p n d", p=128), [], [Bxin[g]], Sx[g]))
        dma("pool", identb[:], din["ident"], [], [Bc["identb"]], Sconstp)
        dma("pool", onesb[:], din["ones"], [], [Bc["onesb"]], Sconstp)
        dma("pool", wsT[:], din["wsT"], [], [Bc["wsT"]], Sconstp)

        act(scondT[:], condT[:], AF.Silu, [Bc["condT"]], [Bc["scondT"]])
        lam_init = [0.8 - 0.6 * math.exp(-0.3 * l) for l in range(L)]

        def lam_setup():
            dve_tt(lamA[:], lamA[:], lamB[:], ALU.mult, [Bc["lamA"], Bc["lamB"]], [Bc["lamA"]])
            P.op("dve", lambda e: e.tensor_reduce(out=lamS[:, 0:4], in_=lamA[:], axis=AX.X, op=ALU.add), reads=[Bc["lamA"]], writes=[Bc["lamS"]])
            act(lamS[:, 0:4], lamS[:, 0:4], AF.Exp, [Bc["lamS"]], [Bc["lamS"]])
            for l in range(L):
                dve_tt(lamS[:, 4 + l:5 + l], lamS[:, 2 * l + 1:2 * l + 2], lamS[:, 2 * l:2 * l + 1], ALU.subtract, [Bc["lamS"]], [Bc["lamS"]])
                dve_ts(lamS[:, 4 + l:5 + l], lamS[:, 4 + l:5 + l], -lam_init[l], None, ALU.add, None, [Bc["lamS"]], [Bc["lamS"]])
                dve_ts(gsub[:, l, :], sublnT[:, l, :], 1.0 - lam_init[l], None, ALU.mult, None, [Bc["sublnT"]], [Bc["gsub"]])

        def x_tiles(tiles):
            for i in tiles:
                g = i // 4
                tt = tt_of_tile(i)
                for half in range(2):
                    b = banks.get()
                    for q in range(4):
                        kc = half * 4 + q
                        tr(ps[b][:, q * 128:(q + 1) * 128], xin_all[:, i, kc * 128:(kc + 1) * 128], ident[:], [Bxin[g], Bc["ident"]], [Bps[b]])
                    dst = xT[:, half * 4:(half + 1) * 4, i * 128:(i + 1) * 128]
                    srcv = ps[b][:].rearrange("p (a t) -> p a t", a=4)
                    wb = [BxT[half * 4 + q][tt] for q in range(4)]
                    if half == 0:
                        act(dst, srcv, AF.Copy, [Bps[b]], wb)
                    else:
                        dve_copy(dst, srcv, [Bps[b]], wb)

        def load_A(src2d, c0, ncols, dst_c0=0, slot=None, halves=(0, 1), extra=()):
            if slot is None:
                slot = next_A()
            srcv = src2d[:, c0:c0 + ncols].rearrange("(k p) f -> p k f", p=128)
            dma("pool", Aslot[slot][:, :, dst_c0:dst_c0 + ncols], srcv, [], [BA[slot][h] for h in halves], SA[slot][halves[0]], extra=extra)
            return slot

        mod_bank = {}
        Bmt = [[Buf(f"mT{l}_{i}") for i in range(3)] for l in range(L)]
        Bmg = [[Buf(f"mG{l}_{i}") for i in range(3)] for l in range(L)]

        def mod_group(l, gq, extra=(), spare=None):
            if l not in mod_bank:
                mod_bank[l] = banks.get(hold=True)
            bm = mod_bank[l]
            if spare is None:
                s_ = load_A(din["w_mod"][l], gq * 512, 512, extra=extra)
                wt, wb = Aslot[s_], None
            else:
                wt, sbuf_, ssem = spare
                dma("pool", wt, din["w_mod"][l][:, gq * 512:(gq + 1) * 512].rearrange("(k p) f -> p k f", p=128), [], [sbuf_], ssem, extra=extra)
                wb = sbuf_
            for jj in range(4):
                j = gq * 4 + jj
                for kc in range(NKC):
                    mm(ps[bm][:, 2 * j:2 * j + 2], wt[:, kc, jj * 128:(jj + 1) * 128], scondT[:, kc, :], kc == 0, kc == NKC - 1,
                       [wb if wb is not None else BA[s_][jj // 2], Bc["scondT"]], [Bps[bm]])

        def mod_finalize(l, i, part="ag"):
            bm = mod_bank[l]
            pv = ps[bm][:, 0:144].rearrange("p (j c) -> p j c", c=2)
            if "a" in part:
                for c in range(2):
                    dve_tt(mT[:, l, 24 * i:24 * i + 16, c], pv[:, 24 * i:24 * i + 16, c], bmodT[:, l, 24 * i:24 * i + 16], ALU.add,
                           [Bps[bm], Bc["bmodT"]], [Bmt[l][i]])
                for c in range(2):
                    dve_stt(aN[:, l, i, :, c], mT[:, l, (3 * i + 1) * 8:(3 * i + 2) * 8, c], 1.0, normgT[:, l, i, :], ALU.add, ALU.mult,
                            [Bmt[l][i], Bc["normgT"]], [Bmt[l][i]])
            if "g" in part:
                for c in range(2):
                    dve_tt(mT[:, l, 24 * i + 16:24 * i + 24, c], pv[:, 24 * i + 16:24 * i + 24, c], bmodT[:, l, 24 * i + 16:24 * i + 24], ALU.add,
                           [Bps[bm], Bc["bmodT"]], [Bmg[l][i]])
                dve_ts(gN[:, l, i, :, :], mT[:, l, (3 * i + 2) * 8:(3 * i + 3) * 8, :], 0.5 if i != 1 else 1.0, None, ALU.mult, None, [Bmg[l][i]], [Bmg[l][i]])
                if i == 2:
                    banks.release(bm)

        def norm_stats(tt, b):
            xb = [BxT[k][tt] for k in range(NKC)]
            hb = [BhT[k][tt] for k in range(NKC)]
            act(hT[:, :, tt * 512:(tt + 1) * 512], xT[:, :, tt * 512:(tt + 1) * 512], AF.Square, xb, hb)
            for kc in range(NKC):
                mm(ps[b][:], onesb[:], hT[:, kc, tt * 512:(tt + 1) * 512], kc == 0, kc == NKC - 1, [Bc["onesb"], BhT[kc][tt]], [Bps[b]])

        def norm_rstd(tt, b):
            act(rstd[:, tt, :], ps[b][:], AF.Ln, [Bps[b]], [Brstd[tt]], bias=EPS, scale=1.0 / D)
            act(rstd[:, tt, :], rstd[:, tt, :], AF.Exp, [Brstd[tt]], [Brstd[tt]], scale=-0.5)

        stat = {"banks": None, "pend": [], "mode": None}

        def stats_begin(mode):
            stat["banks"] = [banks.get(hold=True) for _ in range(NTT)]
            stat["mode"] = mode
            stat["pend"] = []
            stat["n"] = 0

        def stats_update(dc, tt):
            b = stat["banks"][tt]
            if stat["mode"] == "h":
                dst = hT[:, dc, tt * 512:(tt + 1) * 512]
                db = [BhT[dc][tt]]
            else:
                k = stat["n"] % 2
                stat["n"] += 1
                dst = Ebuf[k][:, :]
                db = [BE[k]]
            act(dst, xT[:, dc, tt * 512:(tt + 1) * 512], AF.Square, [BxT[dc][tt]], db)
            stat["pend"].append(lambda: mm(ps[b][:], onesb[:], dst, dc == 0, dc == NKC - 1, [Bc["onesb"]] + db, [Bps[b]]))

        def stats_flush(keep=0):
            while len(stat["pend"]) > keep:
                stat["pend"].pop(0)()

        def norm_phase(l, i_norm, final=False, fr=(), after_tt=None):
            if stat["banks"] is not None:
                stats_flush()
                bks = stat["banks"]
                stat["banks"] = None
                pre_stats = True
            else:
                bks = [banks.get(hold=True) for _ in range(NTT)]
                norm_stats(0, bks[0])
                norm_stats(1, bks[1])
                norm_stats(2, bks[2])
                pre_stats = False
            for tt in range(NTT):
                norm_tt(l, i_norm, tt, bks[tt], final=final)
                if after_tt is not None:
                    after_tt(tt)

        def norm_chunk_ops(l, i_norm, tt):
            c = cond_of_tt(tt)

            def mk(kc):
                def go():
                    s_ = next_scr()
                    dve_stt(scr[:, s_, :], xT[:, kc, tt * 512:(tt + 1) * 512], aN[:, l, i_norm, kc, c:c + 1], rstd[:, tt, :], ALU.mult, ALU.mult,
                            [BxT[kc][tt], Bmt[l][i_norm], Brstd[tt]], [Bscr[s_]])
                    bias = mT[:, l, (3 * i_norm) * 8 + kc, c:c + 1]
                    dst = hT[:, kc, tt * 512:(tt + 1) * 512]
                    if kc == NKC - 1:
                        dve_ts(dst, scr[:, s_, :], bias, None, ALU.add, None, [Bscr[s_], Bmt[l][i_norm]], [BhT[kc][tt]])
                    else:
                        act(dst, scr[:, s_, :], AF.Identity, [Bscr[s_], Bmt[l][i_norm]], [BhT[kc][tt]], bias=bias, scale=1.0)
                return go
            return [mk(kc) for kc in range(NKC)]

        def norm_tt(l, i_norm, tt, bk, final=False):
            norm_rstd(tt, bk)
            banks.release(bk)
            if final:
                return
            for op in norm_chunk_ops(l, i_norm, tt):
                op()

        nq = []

        def norm_drip(k):
            for _ in range(min(k, len(nq))):
                nq.pop(0)()

        SWEEP = [(dc, tt) for dc in range(NKC - 2) for tt in range(NTT)] + [(dc, tt) for tt in range(NTT) for dc in (NKC - 2, NKC - 1)]

        SWEEP_W = [(dc, tt) for dc in range(4) for tt in (0, 1)] + [(dc, 2) for dc in range(4)] \
            + [(dc, tt) for tt in range(NTT) for dc in range(4, NKC)]

        def early_norm(dc, tt, nxt, tail0=NKC - 2):
            if nxt is None or dc != tail0 or tt == 0:
                return
            stats_flush(keep=1)
            norm_rstd(tt - 1, stat["banks"][tt - 1])
            banks.release(stat["banks"][tt - 1])
            nq.extend(norm_chunk_ops(nxt[0], nxt[1], tt - 1))

        def finish_norm(nxt):
            if nxt is None:
                return
            stats_flush()
            norm_drip(len(nq))
            norm_tt(nxt[0], nxt[1], NTT - 1, stat["banks"][NTT - 1])
            stat["banks"] = None

        def ffn_phase(l, which, fence_R, hooks=(), nxt=None):
            hooks = list(hooks)
            gu = din["gu1" if which == 0 else "gu2"][l]
            wd = din["d1" if which == 0 else "d2"][l]
            i_g = 0 if which == 0 else 2
            for grp in range(NJ // 2):
                s = next_A()
                load_A(gu, grp * 256, 256, dst_c0=0, slot=s, halves=(0,))
                load_A(gu, FF + grp * 256, 256, dst_c0=256, slot=s, halves=(1,))
                order = [(jj, tt) for tt in range(NTT) for jj in range(2)] if grp == 0 else [(jj, tt) for jj in range(2) for tt in range(NTT)]
                for (jj, tt) in order:
                    j = grp * 2 + jj
                    if True:
                        bg = banks.get()
                        bu = banks.get()
                        for kc in range(NKC):
                            mm(ps[bg][:], Aslot[s][:, kc, jj * 128:(jj + 1) * 128], hT[:, kc, tt * 512:(tt + 1) * 512], kc == 0, kc == NKC - 1,
                               [BA[s][0], BhT[kc][tt]], [Bps[bg]])
                        for kc in range(NKC):
                            mm(ps[bu][:], Aslot[s][:, kc, 256 + jj * 128:256 + (jj + 1) * 128], hT[:, kc, tt * 512:(tt + 1) * 512], kc == 0, kc == NKC - 1,
                               [BA[s][1], BhT[kc][tt]], [Bps[bu]])
                        sc_ = next_scr()
                        act(scr[:, sc_, :], ps[bg][:], AF.Silu, [Bps[bg]], [Bscr[sc_]])
                        dve_tt(actT[:, j, tt * 512:(tt + 1) * 512], scr[:, sc_, :], ps[bu][:], ALU.mult, [Bscr[sc_], Bps[bu]], [Bact[j][tt]], extra=fence_R)
                if hooks:
                    for hk in hooks.pop(0):
                        hk()
            stats_begin("h")
            bslot = {}

            def load_wd(dc):
                if dc not in bslot:
                    s_ = next_B()
                    srcv = wd[:, dc * 128:(dc + 1) * 128].rearrange("(k p) f -> p k f", p=128)
                    dma("pool", Bslot[s_][:, :, :], srcv, [], [BB[s_]], SB[s_], extra=P.fence(all_mix_B_bufs))
                    bslot[dc] = s_
                return bslot[dc]

            for (dc, tt) in SWEEP:
                s = load_wd(dc)
                if dc == NKC - 2:
                    load_wd(NKC - 1)
                c = cond_of_tt(tt)
                b = banks.get()
                for j in range(NJ):
                    mm(ps[b][:], Bslot[s][:, j, :], actT[:, j, tt * 512:(tt + 1) * 512], j == 0, j == NJ - 1, [BB[s], Bact[j][tt]], [Bps[b]])
                stats_flush(keep=1)
                dve_stt(xT[:, dc, tt * 512:(tt + 1) * 512], ps[b][:], gN[:, l, i_g, dc, c:c + 1], xT[:, dc, tt * 512:(tt + 1) * 512], ALU.mult, ALU.add,
                        [Bps[b], Bmg[l][i_g], BxT[dc][tt]], [BxT[dc][tt]])
                stats_update(dc, tt)
                early_norm(dc, tt, nxt)
                if nxt is not None:
                    norm_drip(4)
                if hooks and ((dc < NKC - 2 and tt == NTT - 1) or dc == NKC - 1):
                    for hk in hooks.pop(0):
                        hk()
            while hooks:
                for hk in hooks.pop(0):
                    hk()
            finish_norm(nxt)

        def rope_tile(src_scr, it, dst_bf, reads, writes, extra=()):
            x4 = scr[:, src_scr, :].rearrange("p (g h f) -> p g h f", g=8, h=2)
            s1 = next_scr()
            s2 = next_scr()
            t1 = scr[:, s1, :].rearrange("p (g h f) -> p g h f", g=8, h=2)
            t2 = scr[:, s2, :].rearrange("p (g h f) -> p g h f", g=8, h=2)
            cb = cosT[:, it:it + 1, :].unsqueeze(1).broadcast_to([128, 8, 2, 32])
            sb3 = sinT[:, it:it + 1, :].broadcast_to([128, 8, 32])
            nsb3 = nsinT[:, it:it + 1, :].broadcast_to([128, 8, 32])
            dve_tt(t1, x4, cb, ALU.mult, reads + [Bc["cosT"]], [Bscr[s1]])
            dve_tt(t2[:, :, 0, :], x4[:, :, 1, :], nsb3, ALU.mult, reads + [Bc["nsinT"]], [Bscr[s2]])
            dve_tt(t2[:, :, 1, :], x4[:, :, 0, :], sb3, ALU.mult, reads + [Bc["sinT"]], [Bscr[s2]])
            dve_tt(dst_bf, scr[:, s1, :], scr[:, s2, :], ALU.add, [Bscr[s1], Bscr[s2]], writes, extra=extra)

        mixpre = {}

        def mixer_pre(l, fence_R):
            fr = fence_R
            w_in = din["w_in"][l]
            sGH = load_A(w_in, 1792, 512)
            mixpre["pre"] = {G: load_A(w_in, c0, 512) for G, c0 in (("q", 0), ("k", 512))}

            def conv_pre(tt):
                for cix in range(2):
                    b = banks.get()
                    for kc in range(NKC):
                        mm(ps[b][:], Aslot[sGH][:, kc, cix * 128:(cix + 1) * 128], hT[:, kc, tt * 512:(tt + 1) * 512], kc == 0, kc == NKC - 1,
                           [BA[sGH][0], BhT[kc][tt]], [Bps[b]])
                    act(xc[:, cix, tt * 512:(tt + 1) * 512], ps[b][:], AF.Copy, [Bps[b]], [Bxc[cix][tt]], extra=fr)
                for cix in range(2):
                    b = banks.get()
                    for kc in range(NKC):
                        mm(ps[b][:], Aslot[sGH][:, kc, 256 + cix * 128:256 + (cix + 1) * 128], hT[:, kc, tt * 512:(tt + 1) * 512], kc == 0, kc == NKC - 1,
                           [BA[sGH][1], BhT[kc][tt]], [Bps[b]])
                    dve_tt(xc[:, cix, tt * 512:(tt + 1) * 512], xc[:, cix, tt * 512:(tt + 1) * 512], ps[b][:], ALU.mult, [Bps[b], Bxc[cix][tt]], [Bxc[cix][tt]])
            return conv_pre

        def mixer_phase(l, fence_R, fence_B):
            fr = fence_R
            fb = fence_B
            w_in = din["w_in"][l]
            for pbi in range(2):
                P.op("dve", lambda e, pbi=pbi: e.memset(qpad[pbi][:, :, :], 0.0), reads=[], writes=[Bqpad[pbi]], extra=fb)
            pre = mixpre.pop("pre")
            ckst = Rg[:, 20480:21504].rearrange("p (c h d) -> p c h d", c=2, h=H)
            for c2 in range(2):
                dma("pool", Vs[:, c2, :].rearrange("p (h d) -> p h d", h=H), din["cv"][l, :, c2 * 128:(c2 + 1) * 128, :].rearrange("h s d -> s h d"),
                    [], [BVs[c2]], Scachev[c2], extra=fr)
            for c2 in range(2):
                dma("pool", ckst[:, c2, :, :], din["ck"][l, :, c2 * 128:(c2 + 1) * 128, :].rearrange("h s d -> s h d"), [], [Bckst[c2]], Sck[c2], extra=fr)

            def cache_k_transposes():
                for c2 in range(2):
                    b = banks.get()
                    pb = ps[b][:].bitcast(BF16)
                    for h in range(H):
                        tr(pb[:, h * 128:(h + 1) * 128], ckst[:, c2, h, :], identb[:], [Bckst[c2], Bc["identb"]], [Bps[b]])
                    act(kTs[:, :, c2 * 128:(c2 + 1) * 128], pb[:, 0:512].rearrange("p (h t) -> p h t", h=H), AF.Copy, [Bps[b]], [BkTc], extra=fr)

            if stop_after == "mix_cache":
                return
            pend = []

            def flush_pend(keep=0):
                while len(pend) > keep:
                    pend.pop(0)()

            def tok_transposes(tb, dstT, cols, wbuf):
                def go():
                    b = banks.get()
                    pb = ps[b][:].bitcast(BF16)
                    for h in range(H):
                        tr(pb[:, h * 128:(h + 1) * 128], tokbf[tb][:, h * 128:(h + 1) * 128], identb[:], [Btok[tb], Bc["identb"]], [Bps[b]])
                    dve_copy(dstT[:, :, cols[0]:cols[1]], pb[:, 0:512].rearrange("p (h t) -> p h t", h=H), [Bps[b]], [wbuf], extra=fr)
                return go

            for G, c0 in (("q", 0), ("k", 512), ("v", 1024), ("uvc", 2304)):
                if stop_after == "mix_G" + G:
                    flush_pend()
                    return
                s = pre.pop(G) if G in pre else load_A(w_in, c0, 512)
                for i in range(NTILE):
                    tt = tt_of_tile(i)
                    b = banks.get()
                    for kc in range(NKC):
                        mm(ps[b][:], hT[:, kc, i * 128:(i + 1) * 128], Aslot[s][:, kc, :], kc == 0, kc == NKC - 1, [BhT[kc][tt], BA[s][0], BA[s][1]], [Bps[b]])
                    flush_pend(keep=1)
                    if G == "k" and i == 1:
                        cache_k_transposes()
                    sample = i >= 4
                    it = i - 4
                    if G == "q":
                        tb = state["tok"]
                        state["tok"] = (tb + 1) % 4
                        if sample:
                            s0 = next_scr()
                            act(scr[:, s0, :], ps[b][:], AF.Copy, [Bps[b]], [Bscr[s0]], scale=0.125)
                            rope_tile(s0, it, tokbf[tb], [Bscr[s0]], [Btok[tb]], extra=fb)
                        else:
                            act(tokbf[tb], ps[b][:], AF.Copy, [Bps[b]], [Btok[tb]], scale=0.125, extra=fb)
                        pend.append(tok_transposes(tb, qT, (i * 128, (i + 1) * 128), BqT[i]))
                    elif G == "k":
                        tb = state["tok"]
                        state["tok"] = (tb + 1) % 4
                        s0 = next_scr()
                        act(scr[:, s0, :], ps[b][:], AF.Copy, [Bps[b]], [Bscr[s0]])
                        if sample:
                            rope_tile(s0, it, tokbf[tb], [Bscr[s0]], [Btok[tb]], extra=fb)
                            pend.append(tok_transposes(tb, kTs, (PAST + it * 128, PAST + (it + 1) * 128), BkT[i]))
                        else:
                            seq, ch = i // 2, i % 2
                            dma("sp", dout["nk"][seq, l, :, ch * 128:(ch + 1) * 128, :].rearrange("h s d -> s h d"),
                                scr[:, s0, :].rearrange("p (h d) -> p h d", h=H), [Bscr[s0]], [], Sout[s0])
                            dve_copy(tokbf[tb], scr[:, s0, :], [Bscr[s0]], [Btok[tb]], extra=fb)
                            pend.append(tok_transposes(tb, kTp, (i * 128, (i + 1) * 128), BkT[i]))
                    elif G == "v":
                        if sample:
                            act(Vs[:, 2 + it, :], ps[b][:], AF.Copy, [Bps[b]], [BVs[2 + it]], extra=fr)
                        else:
                            seq, ch = i // 2, i % 2
                            s0 = next_scr()
                            act(scr[:, s0, :], ps[b][:], AF.Copy, [Bps[b]], [Bscr[s0]])
                            dma("sp", dout["nv"][seq, l, :, ch * 128:(ch + 1) * 128, :].rearrange("h s d -> s h d"),
                                scr[:, s0, :].rearrange("p (h d) -> p h d", h=H), [Bscr[s0]], [], Sout[s0])
                            dve_copy(Vp[:, i, :], scr[:, s0, :], [Bscr[s0]], [BVp[i]], extra=fr)
                    else:
                        act(uvbf[:, i, :], ps[b][:], AF.Copy, [Bps[b]], [Bu[i], Bvc[i]] + (Bckst if i < 2 else []), extra=fr)
            flush_pend()
            if stop_after == "mix_z":
                return

            sG3 = load_A(w_in, 1536, 256)
            gb_sb = rstd[:, :, :].rearrange("p a b -> p (a b)").bitcast(BF16).rearrange("p (c t) -> p c t", c=2)
            for cix in range(2):
                for tt in range(NTT):
                    b = banks.get()
                    for kc in range(NKC):
                        mm(ps[b][:], Aslot[sG3][:, kc, cix * 128:(cix + 1) * 128], hT[:, kc, tt * 512:(tt + 1) * 512], kc == 0, kc == NKC - 1,
                           [BA[sG3][0], BhT[kc][tt]], [Bps[b]])
                    act(gb_sb[:, cix, tt * 512:(tt + 1) * 512], ps[b][:], AF.Copy, [Bps[b]], list(Brstd))
            segs = [(0, 0, 256, False, False), (0, 256, 512, False, False), (1, 512, 1024, False, True), (2, 1024, 1536, True, False)]
            conv_jobs = []

            def conv_job(cix, tt, s0_, s1_, lh, rh):
                def go():
                    w0 = convT[:, l, cix, 0:1]
                    w1 = convT[:, l, cix, 1:2]
                    w2 = convT[:, l, cix, 2:3]
                    n = s1_ - s0_
                    sc_ = next_scr()
                    cv = scr[:, sc_, 0:n]
                    xr = [Bxc[cix][t] for t in range(NTT)]
                    dve_ts(cv, xc[:, cix, s0_:s1_], w1, None, ALU.mult, None, xr + [Bc["convT"]], [Bscr[sc_]])
                    lo = 0 if lh else 1
                    dve_stt(scr[:, sc_, lo:n], xc[:, cix, s0_ + lo - 1:s1_ - 1], w0, scr[:, sc_, lo:n], ALU.mult, ALU.add, xr + [Bc["convT"], Bscr[sc_]], [Bscr[sc_]])
                    hi = n if rh else n - 1
                    dve_stt(scr[:, sc_, 0:hi], xc[:, cix, s0_ + 1:s0_ + 1 + hi], w2, scr[:, sc_, 0:hi], ALU.mult, ALU.add, xr + [Bc["convT"], Bscr[sc_]], [Bscr[sc_]])
                    dve_tt(hT[:, 4 + cix, s0_:s1_], cv, gb_sb[:, cix, s0_:s1_], ALU.mult, [Bscr[sc_]] + list(Brstd), [BhT[4 + cix][tt]])
                return go

            for cix in range(2):
                for (tt, s0_, s1_, lh, rh) in segs:
                    conv_jobs.append(conv_job(cix, tt, s0_, s1_, lh, rh))
            if stop_after == "mix_conv":
                while conv_jobs:
                    conv_jobs.pop(0)()
            if stop_after == "mix_conv":
                return

            def chunk_tail(i, b, yb):
                def go():
                    tt = tt_of_tile(i)
                    b2_ = banks.get()
                    pb = ps[b2_][:].bitcast(BF16)
                    for c2 in range(2):
                        tr(pb[:, c2 * 128:(c2 + 1) * 128], ych[yb][:, c2 * 128:(c2 + 1) * 128], identb[:], [Bych[yb], Bc["identb"]], [Bps[b2_]])
                    act(hT[:, 6:8, i * 128:(i + 1) * 128], pb[:, 0:256].rearrange("p (c t) -> p c t", c=2), AF.Copy, [Bps[b2_]], [BhT[6][tt], BhT[7][tt]])
                return go

            for i in range(NTILE):
                b = banks.get()
                for g in range(4):
                    mm(ps[b][:, g * 64:(g + 1) * 64], wsT[:, l, g, :], uvbf[:, i, 256 + g * 64:256 + (g + 1) * 64], True, True, [Bc["wsT"], Bvc[i]], [Bps[b]])
                flush_pend()
                yb = i % 2
                s_ = next_scr()
                dve_tt(scr[:, s_, 0:256].rearrange("p (g c) -> p g c", g=4), ps[b][:, 0:256].rearrange("p (g c) -> p g c", g=4),
                       cbT[:, l, :].unsqueeze(2).broadcast_to([128, 4, 64]), ALU.add, [Bps[b], Bc["cbT"]], [Bscr[s_]])
                dve_tt(ych[yb][:, 0:256], scr[:, s_, 0:256], uvbf[:, i, 0:256], ALU.mult, [Bscr[s_], Bu[i]], [Bych[yb]], extra=fb)
                pend.append(chunk_tail(i, b, yb))
            flush_pend()
            if stop_after == "mix_chunk":
                return

            nlam = lamS[:, 4 + l:5 + l]
            LA = 3
            blocks = []
            for seq in range(2):
                for h in range(H):
                    blocks.append(dict(h=h, qcols=seq * 256, nq=256, kT=kTp, kcols=seq * 256, Vt=Vp[:, 2 * seq:2 * seq + 2, :],
                                       Vbufs=[BVp[2 * seq], BVp[2 * seq + 1]], kbufs=[BkT[2 * seq], BkT[2 * seq + 1]],
                                       qbufs=[BqT[2 * seq], BqT[2 * seq + 1]], nkc=2, tts=[0]))
            for qt in range(2):
                for h in range(H):
                    blocks.append(dict(h=h, qcols=512 + qt * 512, nq=512, kT=kTs, kcols=0, Vt=Vs, Vbufs=BVs, kbufs=[BkTc] + BkT[4:12],
                                       qbufs=BqT[4 + 4 * qt:8 + 4 * qt], nkc=10, tts=[1 + qt]))
            steps = [(bi, m, kc) for bi, blk in enumerate(blocks) for m in range(2) for kc in range(blk["nkc"])]
            sbank = {}
            ebuf = {}
            accb = {}
            later = []

            def prep(bi):
                blk = blocks[bi]
                pbi = bi % 2
                nq, h, qc = blk["nq"], blk["h"], blk["qcols"]
                dve_copy(qpad[pbi][0:64, 0, 0:nq], qT[0:64, h, qc:qc + nq], blk["qbufs"], [Bqpad[pbi]])
                dve_copy(qpad[pbi][64:128, 1, 0:nq], qT[64:128, h, qc:qc + nq], blk["qbufs"], [Bqpad[pbi]])

            def emit_S(t):
                bi, m, kc = steps[t]
                blk = blocks[bi]
                pbi = bi % 2
                nq, h = blk["nq"], blk["h"]
                b = state["sb"]
                state["sb"] = (b + 1) % 4
                sbank[t] = b
                mm(ps[b][:, 0:nq], blk["kT"][:, h, blk["kcols"] + kc * 128:blk["kcols"] + (kc + 1) * 128], qpad[pbi][:, m, 0:nq], True, True,
                   blk["kbufs"] + [Bqpad[pbi]], [Bps[b]])
                ei = state["E"]
                state["E"] = (ei + 1) % 4
                ebuf[t] = ei
                act(Ebuf[ei][:, 0:nq], ps[b][:, 0:nq], AF.Exp, [Bps[b]], [BE[ei]], extra=fb)

            def abuf(bi):
                nq = blocks[bi]["nq"]
                if nq == 512:
                    k = bi % 2
                    return asc_all[:, k * 512:(k + 1) * 512], sqo_all[:, k * 512:(k + 1) * 512], Bascq[2 * k:2 * k + 2], Bsqo[2 * k:2 * k + 2]
                k = bi % 4
                return asc_all[:, k * 256:(k + 1) * 256], sqo_all[:, k * 256:(k + 1) * 256], [Bascq[k]], [Bsqo[k]]

            def evac(bi, m):
                blk = blocks[bi]
                nq = blk["nq"]
                bo, bs_ = accb[(bi, m)]
                a0, sq_, ab_, sb_ = abuf(bi)
                si = next_scr()
                if nq == 256 or m == 0:
                    act(scr[:, si, 0:nq], ps[bs_][:, 0:nq], AF.Ln, [Bps[bs_]], [Bscr[si]])
                    act(scr[:, si, 0:nq], scr[:, si, 0:nq], AF.Exp, [Bscr[si]], [Bscr[si]], scale=-1.0)
                else:
                    dve_copy(scr[:, si, 0:nq], ps[bs_][:, 0:nq], [Bps[bs_]], [Bscr[si]])
                    dve_recip(scr[:, si, 0:nq], scr[:, si, 0:nq], [Bscr[si]], [Bscr[si]])
                if m == 0:
                    dve_tt(a0, ps[bo][:, 0:nq], scr[:, si, 0:nq], ALU.mult, [Bps[bo], Bscr[si]], ab_, extra=fb)
                else:
                    dve_tt(scr[:, si, 0:nq], ps[bo][:, 0:nq], scr[:, si, 0:nq], ALU.mult, [Bps[bo], Bscr[si]], [Bscr[si]])
                    dve_stt(a0, scr[:, si, 0:nq], nlam, a0, ALU.mult, ALU.add, [Bscr[si], Bc["lamS"]] + ab_, ab_)
                    dve_tt(sq_, a0, a0, ALU.mult, ab_, sb_, extra=fr)

            def finalize(bi, t):
                blk = blocks[bi]
                nq, h, qc = blk["nq"], blk["h"], blk["qcols"]
                a0, sq_, ab_, sb_ = abuf(bi)

                def pe_part():
                    bn = state["sb"]
                    state["sb"] = (bn + 1) % 4
                    mm(ps[bn][:, 0:nq], onesb[:], sq_, True, True, [Bc["onesb"]] + sb_, [Bps[bn]])
                    si = next_scr()
                    act(scr[:, si, 0:nq], ps[bn][:, 0:nq], AF.Ln, [Bps[bn]], [Bscr[si]], bias=EPS, scale=1.0 / 128.0)
                    act(scr[:, si, 0:nq], scr[:, si, 0:nq], AF.Exp, [Bscr[si]], [Bscr[si]], scale=-0.5)
                    dve_stt(hT[:, h, qc:qc + nq], a0, gsub[:, l, h:h + 1], scr[:, si, 0:nq], ALU.mult, ALU.mult,
                            ab_ + [Bc["gsub"], Bscr[si]], [BhT[h][t_] for t_ in blk["tts"]])
                later.append((t + (14 if nq == 512 else 6), pe_part))

            def emit_PV(t):
                bi, m, kc = steps[t]
                blk = blocks[bi]
                nq, h, nkc = blk["nq"], blk["h"], blk["nkc"]
                if kc == 0 and m == 1 and nkc == 2 and bi + 2 < len(blocks):
                    prep(bi + 2)
                if kc == 0 and m == 0 and bi + 1 < len(blocks) and bi >= 1 and blocks[bi - 1]["nkc"] != 2:
                    prep(bi + 1)
                if kc == 0 and m == 0 and nq == 512 and conv_jobs:
                    conv_jobs.pop(0)()
                if kc == 0:
                    aset = state["aset"]
                    state["aset"] = 1 - aset
                    accb[(bi, m)] = (4 + 2 * aset, 5 + 2 * aset)
                bo, bs_ = accb[(bi, m)]
                ei = ebuf[t]
                mm(ps[bo][:, 0:nq], blk["Vt"][:, kc, h * 128:(h + 1) * 128], Ebuf[ei][:, 0:nq], kc == 0, kc == nkc - 1, [blk["Vbufs"][kc], BE[ei]], [Bps[bo]])
                mm(ps[bs_][:, 0:nq], onesb[:], Ebuf[ei][:, 0:nq], kc == 0, kc == nkc - 1, [Bc["onesb"], BE[ei]], [Bps[bs_]])
                if kc == nkc - 1:
                    evac(bi, m)
                    if m == 1:
                        finalize(bi, t)

            nst = len(steps)
            prep(0)
            prep(1)
            for t in range(min(LA, nst)):
                emit_S(t)
            for t in range(nst):
                if t + LA < nst:
                    emit_S(t + LA)
                while later and later[0][0] <= t:
                    later.pop(0)[1]()
                emit_PV(t)
            while conv_jobs:
                conv_jobs.pop(0)()
            banks.rr = 4

            if stop_after == "mix_attn":
                while later:
                    later.pop(0)[1]()
                return
            stats_begin("q")
            aslot = {}
            for gi, (dc, tt) in enumerate(SWEEP_W):
                half, dq = dc // 4, dc % 4
                if half not in aslot:
                    aslot[half] = load_A(din["w_out"][l], half * 512, 512)
                s = aslot[half]
                c = cond_of_tt(tt)
                b = banks.get()
                for kc in range(NKC):
                    mm(ps[b][:], Aslot[s][:, kc, dq * 128:(dq + 1) * 128], hT[:, kc, tt * 512:(tt + 1) * 512], kc == 0, kc == NKC - 1,
                       [BA[s][dq // 2], BhT[kc][tt]], [Bps[b]])
                stats_flush(keep=1)
                dve_stt(xT[:, dc, tt * 512:(tt + 1) * 512], ps[b][:], gN[:, l, 1, dc, c:c + 1], xT[:, dc, tt * 512:(tt + 1) * 512], ALU.mult, ALU.add,
                        [Bps[b], Bmg[l][1], BxT[dc][tt]], [BxT[dc][tt]])
                stats_update(dc, tt)
                early_norm(dc, tt, (l, 2), tail0=4)
                norm_drip(2)
                if gi == 3:
                    while later:
                        later.pop(0)[1]()
            finish_norm((l, 2))

        def final_phase():
            fr = P.fence(all_act_bufs + all_mix_R_bufs)
            outT = [Rg[:, k * 8192:(k + 1) * 8192].bitcast(F32).rearrange("p (k t) -> p k t", k=NKC) for k in range(2)]
            ost = [Rg[:, 16384 + k * 2048:16384 + (k + 1) * 2048].bitcast(F32) for k in range(8)]
            Bout = [[Buf(f"outT{o}_{k}") for k in range(NKC)] for o in range(2)]
            Bost = [[Buf(f"ost{k}_{h}") for h in range(2)] for k in range(8)]
            norm_phase(0, 0, final=True)
            for tt in range(NTT):
                o = tt % 2
                for kc in range(NKC):
                    dve_stt(outT[o][:, kc, :], xT[:, kc, tt * 512:(tt + 1) * 512], fnormT[:, kc:kc + 1], rstd[:, tt, :], ALU.mult, ALU.mult,
                            [BxT[kc][tt], Bc["fnormT"], Brstd[tt]], [Bout[o][kc]], extra=fr)
                for q in range(4):
                    i = tt * 4 + q
                    k8 = i % 8
                    for half in range(2):
                        b = banks.get()
                        for k4 in range(4):
                            kc = half * 4 + k4
                            tr(ps[b][:, k4 * 128:(k4 + 1) * 128], outT[o][:, kc, q * 128:(q + 1) * 128], ident[:], [Bout[o][kc], Bc["ident"]], [Bps[b]])
                        if half == 0:
                            act(ost[k8][:, 0:512], ps[b][:], AF.Copy, [Bps[b]], [Bost[k8][0]], extra=fr)
                        else:
                            act(ost[k8][:, 512:1024], ps[b][:], AF.Copy, [Bps[b]], [Bost[k8][1]], extra=fr)
                    dst = dout["yp"][i * 128:(i + 1) * 128, :] if i < 4 else dout["ys"][(i - 4) * 128:(i - 3) * 128, :]
                    dma("sp", dst, ost[k8], Bost[k8], [], Sost[k8])

        def grp_hook(l, gqs, fins=()):
            return [(lambda l=l, gq=gq: mod_group(l, gq)) for gq in gqs] + [(lambda l=l, i=i: mod_finalize(l, *i) if isinstance(i, tuple) else mod_finalize(l, i)) for i in fins]

        x_tiles(range(0, 8))
        mod_group(0, 0, extra=[xops[1]])
        x_tiles(range(8, 12))
        stat["banks"] = [banks.get(hold=True) for _ in range(NTT)]
        stat["pend"] = []
        for tt in range(NTT):
            norm_stats(tt, stat["banks"][tt])
        mod_group(0, 1)
        for k_, gq in enumerate((2, 3)):
            sp_tile = Rg[:, 24576 + k_ * 4096:24576 + (k_ + 1) * 4096].rearrange("p (k f) -> p k f", k=NKC)
            mod_group(0, gq, spare=(sp_tile, Bmsp[k_], Smsp[k_]))
        mod_finalize(0, 0, "a")
        dump("xT0", xT[:], [b for row in BxT for b in row])
        for l in range(L):
            if l == 0:
                norm_phase(l, 0)
            if l == 0:
                dump("h1", hT[:], [b for row in BhT for b in row])
                hooks1 = [grp_hook(0, [4]), grp_hook(0, [5], fins=[(0, "g")])] + [grp_hook(0, [6 + k]) for k in range(5)] + [grp_hook(0, [11], fins=[1])] \
                    + [grp_hook(0, [12 + k]) for k in range(5)] + [grp_hook(0, [17], fins=[2])]
            else:
                hooks1 = []
            ffn_phase(l, 0, P.fence(all_mix_R_bufs), hooks=hooks1, nxt=(l, 1))
            if l == 0:
                dump("x1", xT[:], [b for row in BxT for b in row])
            if stop_after == "ffn1":
                break
            if l == 0:
                lam_setup()
            fR = P.fence(all_act_bufs)
            conv_pre = mixer_pre(l, fR)
            for tt in range(NTT):
                conv_pre(tt)
            mixer_phase(l, fR, P.fence(BB))
            if l == 0:
                dump("x2", xT[:], [b for row in BxT for b in row])
            if stop_after is not None and stop_after.startswith("mix"):
                break
            if l + 1 < L:
                hooks2 = [grp_hook(l + 1, [k], fins=([0] if k == 5 else [1] if k == 11 else [2] if k == 17 else [])) for k in range(18)]
            else:
                hooks2 = []
            ffn_phase(l, 1, P.fence(all_mix_R_bufs), hooks=hooks2, nxt=((l + 1, 0) if l + 1 < L else None))
            if l == 0:
                dump("x3", xT[:], [b for row in BxT for b in row])
        final_phase()
        counts = P.emit(nc, es, final_waits=out_sems)
    return nc, counts, list(ddbg.keys())


def _rope_tables():
    n_rows = TS // 64
    row = np.repeat(np.arange(n_rows, dtype=np.float32), 64)
    col = np.tile(np.arange(64, dtype=np.float32), n_rows)
    inv = (np.float32(10000.0) ** (-np.arange(16, dtype=np.float32) / np.float32(16))).astype(np.float32)
    ang = np.concatenate([row[:, None] * inv, col[:, None] * inv], axis=-1).astype(np.float32)
    cos = np.cos(ang).astype(np.float32)
    sin = np.sin(ang).astype(np.float32)

    def tm(a):
        return np.ascontiguousarray(a.reshape(8, 128, 32).transpose(1, 0, 2))
    return tm(cos), tm(sin), tm(-sin)


def make_in_maps(x_prompt, x_sample, cache_k, cache_v, c, c_ctx, w_mod, b_mod, norm_g,
                 ffn1_w_gu, ffn1_w_d, ffn2_w_gu, ffn2_w_d, w_in, w_out, attn_lam,
                 attn_subln_g, conv_w, chunk_ws, chunk_b, final_norm_g):
    f = lambda a: np.ascontiguousarray(np.asarray(a, dtype=np.float32))
    cosT, sinT, nsinT = _rope_tables()
    shared = {
        "w_mod": f(w_mod), "gu1": f(ffn1_w_gu), "d1": f(ffn1_w_d), "gu2": f(ffn2_w_gu), "d2": f(ffn2_w_d),
        "w_in": f(w_in), "w_out": f(w_out),
        "bmodT": f(np.asarray(b_mod).reshape(L, 72, 128).transpose(2, 0, 1)),
        "normgT": f(np.asarray(norm_g).reshape(L, 3, NKC, 128).transpose(3, 0, 1, 2)),
        "fnormT": f(np.asarray(final_norm_g).reshape(NKC, 128).transpose(1, 0)),
        "lamA": f(np.broadcast_to(np.asarray(attn_lam)[:, 0::2, :].reshape(1, L * 2, 64), (128, L * 2, 64))),
        "lamB": f(np.broadcast_to(np.asarray(attn_lam)[:, 1::2, :].reshape(1, L * 2, 64), (128, L * 2, 64))),
        "sublnT": f(np.asarray(attn_subln_g).transpose(2, 0, 1)),
        "convT": f(np.asarray(conv_w).reshape(L, 3, 2, 128).transpose(3, 0, 2, 1)),
        "wsT": f(np.asarray(chunk_ws).transpose(3, 0, 1, 2)),
        "cbT": f(np.asarray(chunk_b).transpose(2, 0, 1)),
        "ident": np.eye(128, dtype=np.float32), "ones": np.ones((128, 128), dtype=np.float32),
        "cosT": cosT, "sinT": sinT, "nsinT": nsinT,
    }
    x_prompt = np.asarray(x_prompt, dtype=np.float32)
    x_sample = np.asarray(x_sample, dtype=np.float32)
    cache_k = np.asarray(cache_k, dtype=np.float32)
    cache_v = np.asarray(cache_v, dtype=np.float32)
    c = np.asarray(c, dtype=np.float32)
    c_ctx = np.asarray(c_ctx, dtype=np.float32)
    maps = []
    for cid in range(NCORES):
        cond = np.stack([c_ctx, c[cid]], axis=0)
        m = dict(shared)
        m["xp"] = f(x_prompt[2 * cid:2 * cid + 2].reshape(TP, D))
        m["xs"] = f(x_sample[cid])
        m["ck"] = f(cache_k[cid])
        m["cv"] = f(cache_v[cid])
        m["condT"] = f(cond.reshape(2, NKC, 128).transpose(2, 1, 0))
        maps.append(m)
    return maps


_CACHE = {}


def kernel(**inputs):
    if "nc" not in _CACHE:
        _CACHE["nc"] = build_program()[0]
    nc = _CACHE["nc"]
    maps = make_in_maps(**inputs)
    res = run_bass_kernel_spmd(nc, maps, core_ids=list(range(NCORES)))
    rs = res.results
    y_prompt = np.concatenate([r["yp"].reshape(2, 256, D) for r in rs], axis=0).astype(np.float32)
    y_sample = np.stack([r["ys"] for r in rs], axis=0).astype(np.float32)
    new_k = np.concatenate([r["nk"] for r in rs], axis=0).astype(np.float32)
    new_v = np.concatenate([r["nv"] for r in rs], axis=0).astype(np.float32)
    return (y_prompt, y_sample, new_k, new_v)
```
